# Optimizing a Trainium2 kernel written in Bass

```python
import jax, jax.numpy as jnp
from jax import lax
import numpy as np

D_MODEL = 1024
BATCH = 16
SEQ = 2048
DEPTH = 2

BRANCH_W = 512
N_BRANCH = 3
GM_GROUPS = 4
GM_GROUP_W = BRANCH_W // GM_GROUPS
GM_CHUNK = 128
RET_HEADS = 4
RET_HD = BRANCH_W // RET_HEADS
RET_CHUNK = 128
ROPE_BASE = 10000.0
GDN_HEADS = 4
GDN_HD = BRANCH_W // GDN_HEADS
GDN_CHUNK = 64
CONV_K = 3
EPS = 1e-6

IN_SPLITS = (BRANCH_W,) * 3 + (BRANCH_W,) * 4 + (BRANCH_W,) * 4 + (GDN_HEADS,) * 4
IN_COLS = sum(IN_SPLITS)
IN_OFFSETS = tuple(int(o) for o in np.cumsum(IN_SPLITS)[:-1])

kernel_name = "hybrid_gmlp_retnet_gdn_encoder"

F32 = jnp.float32


def rms_norm(x, g):
    xf = x.astype(F32)
    y = xf * lax.rsqrt(jnp.mean(xf * xf, axis=-1, keepdims=True) + EPS)
    return (y * g.astype(F32)).astype(x.dtype)


def layer_norm(x, g, b):
    xf = x.astype(F32)
    mu = jnp.mean(xf, axis=-1, keepdims=True)
    var = jnp.mean(jnp.square(xf - mu), axis=-1, keepdims=True)
    y = (xf - mu) * lax.rsqrt(var + EPS)
    return (y * g.astype(F32) + b.astype(F32)).astype(x.dtype)


def to_chunks(x, c):
    b, s, h, d = x.shape
    return x.reshape(b, s // c, c, h, d).transpose(1, 0, 3, 2, 4)


def from_chunks(x):
    n, b, h, c, d = x.shape
    return x.transpose(1, 0, 3, 2, 4).reshape(b, n * c, h, d)


def gmlp_branch(u, v, z, ln_g, ln_b, w_s, b_s):
    bsz, seq, _ = u.shape
    u = jax.nn.gelu(u, approximate=False)
    v = layer_norm(jax.nn.gelu(v, approximate=False), ln_g, ln_b)
    vc = v.reshape(bsz, seq // GM_CHUNK, GM_CHUNK, GM_GROUPS, GM_GROUP_W)
    mixed = jnp.einsum("gpq,bnqgc->bnpgc", w_s, vc) + b_s.T[None, None, :, :, None]
    return u * mixed.reshape(bsz, seq, BRANCH_W) * jax.nn.silu(z)


def rotary(x):
    seq, hd = x.shape[1], x.shape[-1]
    inv = ROPE_BASE ** (-jnp.arange(0, hd, 2, dtype=F32) / hd)
    ang = jnp.arange(seq, dtype=F32)[:, None] * inv[None, :]
    cos = jnp.cos(ang)[None, :, None, :]
    sin = jnp.sin(ang)[None, :, None, :]
    x1, x2 = x[..., : hd // 2], x[..., hd // 2:]
    return jnp.concatenate([x1 * cos - x2 * sin, x2 * cos + x1 * sin], axis=-1)


def retention_branch(q, k, v, z, decay_logit, norm_g):
    bsz, seq, _ = q.shape
    shp = (bsz, seq, RET_HEADS, RET_HD)
    q = rotary(q.reshape(shp).astype(F32))
    k = rotary(k.reshape(shp).astype(F32)) * (RET_HD ** -0.5)
    v = v.reshape(shp).astype(F32)
    log_g = jax.nn.log_sigmoid(decay_logit.astype(F32))
    lf = log_g[0][:, None]
    lb = log_g[1][:, None]
    pos = jnp.arange(RET_CHUNK, dtype=F32)[None, :]
    rel = pos.T - pos
    mask = jnp.exp(jnp.where(rel[None] >= 0, lf[:, :, None] * rel[None], -lb[:, :, None] * rel[None]))
    q_dec_f = jnp.exp(lf * (pos + 1.0))[None, :, :, None]
    k_dec_f = jnp.exp(lf * (RET_CHUNK - 1.0 - pos))[None, :, :, None]
    q_dec_b = jnp.exp(lb * (RET_CHUNK - pos))[None, :, :, None]
    k_dec_b = jnp.exp(lb * pos)[None, :, :, None]
    chunk_f = jnp.exp(lf * RET_CHUNK)[None, :, :, None]
    chunk_b = jnp.exp(lb * RET_CHUNK)[None, :, :, None]
    qc, kc, vc = to_chunks(q, RET_CHUNK), to_chunks(k, RET_CHUNK), to_chunks(v, RET_CHUNK)
    state0 = jnp.zeros((bsz, RET_HEADS, RET_HD, RET_HD), F32)

    def fwd_step(state, inp):
        qi, ki, vi = inp
        scores = jnp.einsum("bhqd,bhkd->bhqk", qi, ki) * mask[None]
        out = (jnp.einsum("bhqk,bhke->bhqe", scores, vi)
               + jnp.einsum("bhcd,bhde->bhce", qi * q_dec_f, state))
        state = state * chunk_f + jnp.einsum("bhcd,bhce->bhde", ki * k_dec_f, vi)
        return state, out

    def bwd_step(state, inp):
        qi, ki, vi = inp
        out = jnp.einsum("bhcd,bhde->bhce", qi * q_dec_b, state)
        state = state * chunk_b + jnp.einsum("bhcd,bhce->bhde", ki * k_dec_b, vi)
        return state, out

    _, o_f = lax.scan(fwd_step, state0, (qc, kc, vc))
    _, o_b = lax.scan(bwd_step, state0, (qc, kc, vc), reverse=True)
    o = from_chunks(o_f + o_b)
    mu = jnp.mean(o, axis=-1, keepdims=True)
    o = (o - mu) * lax.rsqrt(jnp.mean(jnp.square(o - mu), axis=-1, keepdims=True) + EPS)
    o = o.reshape(bsz, seq, BRANCH_W) * norm_g.astype(F32)
    return o.astype(z.dtype) * jax.nn.silu(z)


def centred_depthwise_conv(x, w):
    ch = x.shape[-1]
    pad = (CONV_K - 1) // 2
    return lax.conv_general_dilated(
        x, w[:, None, :].astype(x.dtype), window_strides=(1,), padding=[(pad, pad)],
        dimension_numbers=("NWC", "WIO", "NWC"), feature_group_count=ch)


def l2_normalize(x):
    return x * lax.rsqrt(jnp.sum(x * x, axis=-1, keepdims=True) + EPS)


def gdn_direction(q, k, v, beta, g):
    bsz = q.shape[0]
    c = GDN_CHUNK
    qc, kc, vc = to_chunks(q, c), to_chunks(k, c), to_chunks(v, c)
    bc = to_chunks(beta[..., None], c)[..., 0]
    gc = jnp.cumsum(to_chunks(g[..., None], c)[..., 0], axis=-1)
    idx = jnp.arange(c)
    incl = idx[:, None] >= idx[None, :]
    strict = idx[:, None] > idx[None, :]
    decay = jnp.exp(jnp.where(incl, gc[..., :, None] - gc[..., None, :], -jnp.inf))
    kb = kc * bc[..., None]
    lower = jnp.where(strict, jnp.einsum("nbhid,nbhjd->nbhij", kb, kc) * decay, 0.0)
    eye = jnp.eye(c, dtype=F32)
    rhs = jnp.concatenate([vc * bc[..., None], kb * jnp.exp(gc)[..., None]], axis=-1)
    sol = lax.linalg.triangular_solve(lower + eye, rhs, left_side=True, lower=True)
    u, w = sol[..., :GDN_HD], sol[..., GDN_HD:]

    def step(state, inp):
        qi, ki, ui, wi, gi, di = inp
        v_new = ui - jnp.einsum("bhcd,bhde->bhce", wi, state)
        scores = jnp.einsum("bhqd,bhkd->bhqk", qi, ki) * di
        out = (jnp.einsum("bhcd,bhde->bhce", qi * jnp.exp(gi)[..., None], state)
               + jnp.einsum("bhqk,bhke->bhqe", scores, v_new))
        g_last = gi[..., -1:]
        state = (state * jnp.exp(g_last)[..., None]
                 + jnp.einsum("bhcd,bhce->bhde", ki * jnp.exp(g_last - gi)[..., None], v_new))
        return state, out

    state0 = jnp.zeros((bsz, GDN_HEADS, GDN_HD, GDN_HD), F32)
    _, o = lax.scan(step, state0, (qc, kc, u, w, gc, decay))
    return from_chunks(o)


def gdn_branch(q, k, v, z, a_f, b_f, a_b, b_b, conv_w, a_log, dt_bias, norm_g):
    bsz, seq, _ = q.shape
    qkv = jax.nn.silu(centred_depthwise_conv(jnp.concatenate([q, k, v], axis=-1), conv_w))
    q, k, v = jnp.split(qkv.astype(F32), 3, axis=-1)
    shp = (bsz, seq, GDN_HEADS, GDN_HD)
    q = l2_normalize(q.reshape(shp)) * (GDN_HD ** -0.5)
    k = l2_normalize(k.reshape(shp))
    v = v.reshape(shp)
    a_rate = jnp.exp(a_log.astype(F32))
    dtb = dt_bias.astype(F32)
    g_f = -a_rate[0] * jax.nn.softplus(a_f.astype(F32) + dtb[0])
    g_b = -a_rate[1] * jax.nn.softplus(a_b.astype(F32) + dtb[1])
    beta_f = jax.nn.sigmoid(b_f.astype(F32))
    beta_b = jax.nn.sigmoid(b_b.astype(F32))
    o_f = gdn_direction(q, k, v, beta_f, g_f)
    rev = lambda t: jnp.flip(t, axis=1)
    o_b = rev(gdn_direction(rev(q), rev(k), rev(v), rev(beta_b), rev(g_b)))
    o = o_f + o_b
    o = o * lax.rsqrt(jnp.mean(o * o, axis=-1, keepdims=True) + EPS) * norm_g.astype(F32)
    return o.reshape(bsz, seq, BRANCH_W).astype(z.dtype) * jax.nn.silu(z)


def setup_inputs(seed: int = 0) -> dict:
    key = jax.random.key(seed)
    ks = jax.random.split(key, 17)
    nrm = lambda k, shape, scale: jax.random.normal(k, shape, F32) * scale
    x = nrm(ks[0], (BATCH, SEQ, D_MODEL), 1.0)
    norm_g = 1.0 + nrm(ks[1], (DEPTH, D_MODEL), 0.05)
    w_in = nrm(ks[2], (DEPTH, D_MODEL, IN_COLS), D_MODEL ** -0.5)
    gm_ln_g = 1.0 + nrm(ks[3], (DEPTH, BRANCH_W), 0.05)
    gm_ln_b = nrm(ks[4], (DEPTH, BRANCH_W), 0.02)
    gm_w_s = nrm(ks[5], (DEPTH, GM_GROUPS, GM_CHUNK, GM_CHUNK), GM_CHUNK ** -0.5)
    gm_b_s = 1.0 + nrm(ks[6], (DEPTH, GM_GROUPS, GM_CHUNK), 0.05)
    base_gamma = 1.0 - 2.0 ** (-5.0 - np.arange(RET_HEADS))
    base_logit = np.log(base_gamma / (1.0 - base_gamma)).astype(np.float32)
    ret_decay_logit = jnp.asarray(base_logit)[None, None, :] + nrm(ks[7], (DEPTH, 2, RET_HEADS), 0.1)
    ret_norm_g = 1.0 + nrm(ks[8], (DEPTH, BRANCH_W), 0.05)
    gdn_conv_w = nrm(ks[9], (DEPTH, CONV_K, 3 * BRANCH_W), CONV_K ** -0.5)
    gdn_a_log = jnp.log(jax.random.uniform(ks[10], (DEPTH, 2, GDN_HEADS), F32, 1.0, 16.0))
    dt = jnp.exp(jax.random.uniform(ks[11], (DEPTH, 2, GDN_HEADS), F32,
                                    float(np.log(1e-3)), float(np.log(1e-1))))
    gdn_dt_bias = dt + jnp.log(-jnp.expm1(-dt))
    gdn_norm_g = 1.0 + nrm(ks[12], (DEPTH, GDN_HD), 0.05)
    w_gate = nrm(ks[13], (DEPTH, D_MODEL, N_BRANCH, D_MODEL), D_MODEL ** -0.5)
    w_branch_out = nrm(ks[14], (DEPTH, N_BRANCH, BRANCH_W, D_MODEL), BRANCH_W ** -0.5)
    w_out = nrm(ks[15], (DEPTH, D_MODEL, D_MODEL), D_MODEL ** -0.5)
    final_norm_g = 1.0 + nrm(ks[16], (D_MODEL,), 0.05)
    return {"x": x, "norm_g": norm_g, "w_in": w_in, "gm_ln_g": gm_ln_g, "gm_ln_b": gm_ln_b,
            "gm_w_s": gm_w_s, "gm_b_s": gm_b_s, "ret_decay_logit": ret_decay_logit,
            "ret_norm_g": ret_norm_g, "gdn_conv_w": gdn_conv_w, "gdn_a_log": gdn_a_log,
            "gdn_dt_bias": gdn_dt_bias, "gdn_norm_g": gdn_norm_g, "w_gate": w_gate,
            "w_branch_out": w_branch_out, "w_out": w_out, "final_norm_g": final_norm_g}


def reference(x, norm_g, w_in, gm_ln_g, gm_ln_b, gm_w_s, gm_b_s, ret_decay_logit, ret_norm_g,
              gdn_conv_w, gdn_a_log, gdn_dt_bias, gdn_norm_g, w_gate, w_branch_out, w_out,
              final_norm_g):
    for l in range(DEPTH):
        h = rms_norm(x, norm_g[l])
        (gu, gv, gz, rq, rk, rv, rz, dq, dk, dv, dz,
         a_f, b_f, a_b, b_b) = jnp.split(h @ w_in[l], IN_OFFSETS, axis=-1)
        y_a = gmlp_branch(gu, gv, gz, gm_ln_g[l], gm_ln_b[l], gm_w_s[l], gm_b_s[l])
        y_b = retention_branch(rq, rk, rv, rz, ret_decay_logit[l], ret_norm_g[l])
        y_c = gdn_branch(dq, dk, dv, dz, a_f, b_f, a_b, b_b, gdn_conv_w[l], gdn_a_log[l],
                         gdn_dt_bias[l], gdn_norm_g[l])
        branches = jnp.stack([y_a, y_b, y_c], axis=2)
        proj = jnp.einsum("bsnw,nwd->bsnd", branches, w_branch_out[l])
        gates = jax.nn.sigmoid(jnp.einsum("bsd,dne->bsne", h, w_gate[l]))
        merged = jnp.sum(gates * proj, axis=2)
        x = x + merged @ w_out[l]
    return rms_norm(x, final_norm_g)
```

```python
import contextlib
import numpy as np
import concourse.bass as bass
import concourse.mybir as mybir
from concourse.bass_utils import run_bass_kernel_spmd

F32 = mybir.dt.float32
BF16 = mybir.dt.bfloat16
ALU = mybir.AluOpType
AF = mybir.ActivationFunctionType

D_MODEL = 1024
IN_COLS = 5648
EPS = 1e-6
NEG = -1.0e30


class V:
    __slots__ = ("ap", "key", "gen", "hk")

    def __init__(self, ap, key=None, gen=None, hk=None):
        self.ap = ap
        self.key = key if key is not None else object()
        self.gen = gen
        self.hk = hk

    def h(self, i):
        return V(self.ap[:, i], self.hk[i] if self.hk is not None else self.key, self.gen)

    def __getitem__(self, idx):
        return V(self.ap[idx], self.key, self.gen)

    def re(self, s, **kw):
        return V(self.ap.rearrange(s, **kw), self.key, self.gen)

    def bc(self, shape, axes=()):
        ap = self.ap
        for a in axes:
            ap = ap.unsqueeze(a)
        return V(ap.to_broadcast(list(shape)), self.key, self.gen)

    def bitcast(self, dt):
        return V(self.ap.bitcast(dt), self.key, self.gen)

    @property
    def shape(self):
        return self.ap.shape


class KS(tuple):
    pass


def _keys(vs):
    out = []
    for v in vs:
        if isinstance(v, V):
            k = v.key
            if isinstance(k, KS):
                out.extend(k)
            else:
                out.append(k)
    return out


class _Op:
    __slots__ = ("eng", "fn", "dma", "deps", "sig", "need_sig", "pre")

    def __init__(self, eng, fn, dma):
        self.eng = eng
        self.fn = fn
        self.dma = dma
        self.deps = []
        self.sig = None
        self.need_sig = dma
        self.pre = None


ENGS = ("pe", "act", "dve", "pool", "sp")
NDMASEM = 16


class Prog:
    def __init__(self, nc):
        self.nc = nc
        self.ops = []
        self.last_w = {}
        self.readers = {}
        self.bank_gen = {}
        self.excl = set()

    def _rec(self, eng, fn, reads, writes, dma=False):
        op = _Op(eng, fn, dma)
        for v in list(reads) + list(writes):
            if isinstance(v, V) and v.gen is not None:
                assert self.bank_gen[v.gen[0]] == v.gen[1], ("stale tile use (buffer re-allocated)", v.gen[0])
        rk = _keys(reads)
        wk = _keys(writes)
        deps = set()
        for k in rk:
            w = self.last_w.get(k)
            if w is not None:
                deps.add(w)
            if k in self.excl:
                rd = self.readers.get(k)
                if rd:
                    for e2, o2 in rd.items():
                        if e2 != eng:
                            deps.add(o2)
        for k in wk:
            w = self.last_w.get(k)
            if w is not None:
                deps.add(w)
            rd = self.readers.get(k)
            if rd:
                deps.update(rd.values())
        for d in deps:
            if d.eng == "pe" and eng == "pe" and not d.dma and not dma:
                continue
            d.need_sig = True
            op.deps.append(d)
        for k in rk:
            rd = self.readers.setdefault(k, {})
            rd[id(op) if dma else eng] = op
        for k in wk:
            self.last_w[k] = op
            self.readers[k] = {}
        self.ops.append(op)
        return op

    def mm(self, out, lhsT, rhs, start=True, stop=True):
        self._rec("pe", lambda e: e.matmul(out.ap, lhsT.ap, rhs.ap, start=start, stop=stop),
                  [lhsT, rhs], [out])

    def tr(self, out, in_, ident):
        self._rec("pe", lambda e: e.transpose(out.ap, in_.ap, ident.ap), [in_, ident], [out])

    def act(self, out, in_, func, bias=None, scale=None, accum=None):
        kw = {}
        reads = [in_]
        writes = [out]
        if bias is not None:
            kw["bias"] = bias.ap if isinstance(bias, V) else bias
            reads.append(bias)
        if scale is not None:
            kw["scale"] = scale.ap if isinstance(scale, V) else scale
            reads.append(scale)
        if accum is not None:
            kw["accum_out"] = accum.ap
            writes.append(accum)
        self._rec("act", lambda e: e.activation(out.ap, in_.ap, func, **kw), reads, writes)

    def tt(self, eng, out, a, b, op):
        self._rec(eng, lambda e: e.tensor_tensor(out.ap, a.ap, b.ap, op), [a, b], [out])

    def ts(self, eng, out, a, s1, op0, s2=None, op1=None):
        s1a = s1.ap if isinstance(s1, V) else s1
        s2a = s2.ap if isinstance(s2, V) else s2
        kw = {}
        if op1 is not None:
            kw["op1"] = op1
        self._rec(eng, lambda e: e.tensor_scalar(out.ap, a.ap, s1a, s2a, op0, **kw),
                  [a, s1, s2], [out])

    def stt(self, eng, out, a, s, b, op0, op1):
        sa = s.ap if isinstance(s, V) else s
        self._rec(eng, lambda e: e.scalar_tensor_tensor(out.ap, a.ap, sa, b.ap, op0, op1),
                  [a, s, b], [out])

    def copy(self, eng, out, a):
        if eng == "act":
            self._rec("act", lambda e: e.copy(out.ap, a.ap), [a], [out])
        else:
            self._rec(eng, lambda e: e.tensor_copy(out.ap, a.ap), [a], [out])

    def memset(self, eng, out, val):
        self._rec(eng, lambda e: e.memset(out.ap, val), [], [out])

    def dma(self, q, out, in_, **kw):
        self._rec(q, lambda e: e.dma_start(out=out.ap, in_=in_.ap, **kw), [in_], [out], dma=True)

    def emit(self, stack):
        nc = self.nc
        sems = {}
        for e in ("pe", "act", "dve", "pool"):
            sems[e] = stack.enter_context(nc.semaphore("s_" + e))
        dsem = {}
        for q in ("sp", "act", "pool"):
            dsem[q] = [stack.enter_context(nc.semaphore("d_%s%d" % (q, i))) for i in range(NDMASEM)]
        cnt = {e: 0 for e in sems}
        dn = {q: 0 for q in dsem}
        dcnt = {q: [0] * NDMASEM for q in dsem}
        for op in self.ops:
            if op.dma:
                q = op.eng
                slot = dn[q] % NDMASEM
                dn[q] += 1
                prev = dcnt[q][slot]
                dcnt[q][slot] += 16
                op.sig = (dsem[q][slot], dcnt[q][slot])
                op.pre = (dsem[q][slot], prev) if prev > 0 else None
            elif op.need_sig:
                cnt[op.eng] += 1
                op.sig = (sems[op.eng], cnt[op.eng])
        per = {e: [] for e in ENGS}
        for op in self.ops:
            per[op.eng].append(op)
        finals = []
        for q in dsem:
            for i in range(NDMASEM):
                if dcnt[q][i] > 0:
                    finals.append((dsem[q][i], dcnt[q][i]))
        self.stats = {e: len(per[e]) for e in ENGS}

        def run(engname, e):
            waited = {}
            for op in per[engname]:
                need = {}
                if op.pre is not None:
                    need[id(op.pre[0])] = op.pre
                for d in op.deps:
                    s, v = d.sig
                    if need.get(id(s), (None, 0))[1] < v:
                        need[id(s)] = (s, v)
                for sid, (s, v) in need.items():
                    if waited.get(sid, 0) >= v:
                        continue
                    waited[sid] = v
                    e.wait_ge(s, v)
                ins = op.fn(e)
                if op.sig is not None:
                    ins.then_inc(op.sig[0], 16 if op.dma else 1)
            if engname == "sp":
                for (s, v) in finals:
                    e.wait_ge(s, v)

        block = stack.enter_context(nc.Block())

        @block.tensor
        def _(e):
            run("pe", e)

        @block.scalar
        def _(e):
            run("act", e)

        @block.vector
        def _(e):
            run("dve", e)

        @block.gpsimd
        def _(e):
            run("pool", e)

        @block.sync
        def _(e):
            run("sp", e)


C_ID, C_TRIF, C_TRIB, C_NEGF, C_NEGB, C_OFFD, C_PM, C_QM = [i * 128 for i in range(8)]
C_POSC = 1024
C_POSR = 1032
C_COS = C_POSR + 512
def _const_cols(nch):
    return C_COS + 3 * nch * 64


def make_consts(nch):
    j = np.arange(128)[:, None].astype(np.float64)
    i = np.arange(128)[None, :].astype(np.float64)
    c = np.zeros((128, _const_cols(nch)), np.float32)
    c[:, C_ID:C_ID + 128] = (j == i)
    c[:, C_TRIF:C_TRIF + 128] = (j <= i)
    c[:, C_TRIB:C_TRIB + 128] = (j >= i)
    c[:, C_NEGF:C_NEGF + 128] = np.where(i >= j, 0.0, NEG)
    c[:, C_NEGB:C_NEGB + 128] = np.where(i <= j, 0.0, NEG)
    c[:, C_OFFD:C_OFFD + 128] = (j != i)
    c[:, C_PM:C_PM + 128] = np.maximum(i - j, 0)
    c[:, C_QM:C_QM + 128] = np.maximum(j - i, 0)
    p = np.arange(128).astype(np.float64)
    c[:, C_POSC + 0] = p + 1
    c[:, C_POSC + 1] = 127 - p
    c[:, C_POSC + 2] = 128 - p
    c[:, C_POSC + 3] = p
    c[:, C_POSR + 0:C_POSR + 128] = (p + 1)[None, :]
    c[:, C_POSR + 128:C_POSR + 256] = (127 - p)[None, :]
    c[:, C_POSR + 256:C_POSR + 384] = (128 - p)[None, :]
    c[:, C_POSR + 384:C_POSR + 512] = p[None, :]
    inv = (10000.0 ** (-np.arange(0, 128, 2, dtype=np.float32) / np.float32(128))).astype(np.float32)
    pos = (np.arange(nch)[None, :] * 128 + np.arange(128)[:, None]).astype(np.float32)
    ang = (pos[:, :, None] * inv[None, None, :]).astype(np.float32)
    cos = np.cos(ang).astype(np.float32).reshape(128, nch * 64)
    sin = np.sin(ang).astype(np.float32).reshape(128, nch * 64)
    c[:, C_COS:C_COS + nch * 64] = cos
    c[:, C_COS + nch * 64:C_COS + 2 * nch * 64] = sin
    c[:, C_COS + 2 * nch * 64:C_COS + 3 * nch * 64] = -sin
    return c


def build(nseq, nch, depth):
    CHAIN_LSW = 5
    PREC = 0
    S = nch * 128
    NT = nseq * nch
    nc = bass.Bass("TRN2", target_bir_lowering=False)
    st = contextlib.ExitStack()

    def din(name, shape, dt=F32):
        return V(nc.dram_tensor(name, list(shape), dt, kind="ExternalInput").ap())

    def dscr(name, shape, dt):
        return V(nc.dram_tensor(name, list(shape), dt, kind="Internal").ap())

    x_d = din("x", [NT, 128, D_MODEL])
    out_d = V(nc.dram_tensor("out", [NT, 128, D_MODEL], F32, kind="ExternalOutput").ap())
    consts_d = din("consts", [128, _const_cols(nch)])
    norm_g_d = din("norm_g", [depth, D_MODEL])
    w_in_d = din("w_in", [depth, D_MODEL, IN_COLS])
    gm_ln_g_d = din("gm_ln_g", [depth, 512])
    gm_ln_b_d = din("gm_ln_b", [depth, 512])
    gm_w_s_d = din("gm_w_s", [depth, 4, 128, 128])
    gm_b_s_d = din("gm_b_s", [depth, 512])
    ret_dl_d = din("ret_decay_logit", [depth, 8])
    ret_ng_d = din("ret_norm_g", [depth, 512])
    conv_w_d = din("gdn_conv_w", [depth, 3, 1536])
    a_log_d = din("gdn_a_log", [depth, 8])
    dtb_d = din("gdn_dt_bias", [depth, 8])
    gdn_ng_d = din("gdn_norm_g", [depth, 128])
    w_gate_d = din("w_gate", [depth, D_MODEL, 3072])
    w_bo_d = din("w_branch_out", [depth, 1536, D_MODEL])
    w_out_d = din("w_out", [depth, D_MODEL, D_MODEL])
    fng_d = din("final_norm_g", [1, D_MODEL])

    xres_d = dscr("xres", [NT, 128, D_MODEL], F32)
    Y_d = [dscr("y%d" % n, [nch, 128, 512], BF16) for n in range(3)]
    ACC_d = dscr("accm", [nch, 128, 1024], F32)
    RZ_d = dscr("rz", [nch, 128, 512], BF16)
    GZ_d = dscr("gz", [nch, 128, 512], BF16)
    ROF_d = dscr("rof", [nch, 128, 512], F32)
    GOF_d = dscr("gof", [nch, 128, 512], F32)
    RQDB_d = dscr("rqdb", [nch, 128, 512], BF16)
    RKDB_d = dscr("rkdb", [nch, 128, 512], BF16)
    RV_d = dscr("rv", [nch, 128, 512], BF16)
    GKQ_d = dscr("gkq", [nch, 128, 1024], BF16)
    GVT_d = dscr("gvt", [nch, 128, 512], BF16)
    GKDB_d = dscr("gkdb", [nch, 128, 512], BF16)
    GSC_d = dscr("gsc", [nch, 128, 56], F32)

    def heads(v, A):
        ks = [object() for _ in range(A)]
        return V(v.ap, KS(ks), None, [KS((k,)) for k in ks])

    def sb(name, shape, dt=F32):
        return V(st.enter_context(nc.sbuf_tensor("sb_" + name, list(shape), dt))[:])

    P = Prog(nc)
    banks = [V(st.enter_context(nc.psum_tensor("ps%d" % i, [128, 512], F32))[:]) for i in range(8)]
    bank_i = [0]
    for b in banks:
        P.excl.add(b.key)

    ALLB = {"banks": list(range(8)), "i": 0}
    gen_ctr = [0]

    def bgroup(ids):
        return {"banks": list(ids), "i": 0}

    def ps(shape, dt=F32, g=None):
        g = g or ALLB
        bi = g["banks"][g["i"] % len(g["banks"])]
        g["i"] += 1
        gen_ctr[0] += 1
        P.bank_gen[bi] = gen_ctr[0]
        b = V(banks[bi].ap, banks[bi].key, (bi, gen_ctr[0]))
        if dt == BF16:
            b = b.bitcast(BF16)
        n = int(np.prod(shape[1:]))
        v = b[:, 0:n]
        if len(shape) == 3:
            v = v.re("p (a b) -> p a b", a=shape[1])
        elif len(shape) == 4:
            v = v.re("p (a b c) -> p a b c", a=shape[1], b=shape[2])
        return v

    rings = {}
    NU = 86
    ARENA = st.enter_context(nc.sbuf_tensor("sb_arena", [128, NU * 256], F32))[:]
    ukeys = [("U", i) for i in range(NU * 4)]
    bump = [0]

    limit = [NU]

    def new_phase(reserve=0):
        for k in [k for k in rings if not k.startswith("!")]:
            del rings[k]
        bump[0] = 0
        limit[0] = (NU - reserve) * 4

    def tmp(name, shape, dt=F32, n=1):
        nel = int(np.prod(shape[1:]))
        nbytes = nel * (4 if dt == F32 else 2)
        if nbytes < 512:
            name = "!" + name
            if name not in rings:
                rings[name] = [[sb("%s_%d" % (name[1:], i), shape, dt) for i in range(n)], 0]
        elif name not in rings:
            units = (nbytes + 255) // 256
            bufs = []
            for i in range(n):
                u0 = bump[0]
                bump[0] += units
                assert bump[0] <= limit[0], ("arena overflow", name, bump[0], limit[0])
                v = V(ARENA[:, u0 * 64:(u0 + units) * 64], KS(ukeys[u0:u0 + units]))
                if dt != F32:
                    v = v.bitcast(BF16)
                v = v[:, 0:nel]
                if len(shape) == 3:
                    v = v.re("p (a b) -> p a b", a=shape[1])
                elif len(shape) == 4:
                    v = v.re("p (a b c) -> p a b c", a=shape[1], b=shape[2])
                if len(shape) >= 3 and (nbytes // shape[1]) % 256 == 0 and nbytes % shape[1] == 0:
                    k = (nbytes // shape[1]) // 256
                    v.hk = [KS(ukeys[u0 + j * k:u0 + (j + 1) * k]) for j in range(shape[1])]
                bufs.append(v)
            rings[name] = [bufs, 0]
        r = rings[name]
        i = r[1] % len(r[0])
        v = r[0][i]
        r[1] += 1
        gen_ctr[0] += 1
        gid = ("ring", name, i)
        P.bank_gen[gid] = gen_ctr[0]
        return V(v.ap, v.key, (gid, gen_ctr[0]), v.hk)

    def run_window(genfn, items, K, lag):
        groups = [bgroup(range(i * (8 // K), (i + 1) * (8 // K))) for i in range(K)]
        items = list(items)
        active = []
        nxt = 0
        while active or nxt < len(items):
            if nxt < len(items) and len(active) < K and (not active or active[-1][1] >= lag):
                active.append([genfn(items[nxt], groups[nxt % K]), 0])
                nxt += 1
            for a in list(active):
                try:
                    next(a[0])
                    a[1] += 1
                except StopIteration:
                    active.remove(a)

    def run_gens(gens):
        gens = list(gens)
        while gens:
            for g in list(gens):
                try:
                    next(g)
                except StopIteration:
                    gens.remove(g)

    CT = sb("consts", [128, _const_cols(nch)])
    P.dma("sp", CT, consts_d)
    IDF = CT[:, C_ID:C_ID + 128]
    TRIF = CT[:, C_TRIF:C_TRIF + 128]
    TRIB = CT[:, C_TRIB:C_TRIB + 128]
    NEGM = [CT[:, C_NEGF:C_NEGF + 128], CT[:, C_NEGB:C_NEGB + 128]]
    OFFD = CT[:, C_OFFD:C_OFFD + 128]
    PM = CT[:, C_PM:C_PM + 128]
    QM = CT[:, C_QM:C_QM + 128]
    POSC = CT[:, C_POSC:C_POSC + 4]
    POSR = CT[:, C_POSR:C_POSR + 512].re("p (a b) -> p a b", a=4)
    COS = CT[:, C_COS:C_COS + nch * 64].re("p (c f) -> p c f", c=nch)
    SIN = CT[:, C_COS + nch * 64:C_COS + 2 * nch * 64].re("p (c f) -> p c f", c=nch)
    NSIN = CT[:, C_COS + 2 * nch * 64:C_COS + 3 * nch * 64].re("p (c f) -> p c f", c=nch)
    IDB = sb("idb", [128, 128], BF16)
    P.copy("dve", IDB, IDF)
    ONESF = sb("onesf", [128, 128])
    P.memset("pool", ONESF, 1.0)
    ONESB = sb("onesb", [128, 128], BF16)
    P.memset("pool", ONESB, 1.0)
    ONESDIV = sb("onesdiv", [128, 128])
    P.memset("pool", ONESDIV, 1.0 / 128)

    hT = sb("hT", [128, 8, S + 2], BF16)
    P.memset("pool", hT[:, :, 0:1], 0.0)
    P.memset("pool", hT[:, :, S + 1:S + 2], 0.0)
    WBUF = sb("wbuf", [128, 8 * 2560], BF16)

    NG = sb("ng", [128, 1024])
    FNG = sb("fng", [128, 1024])
    LNG = sb("lng", [128, 512])
    LNB = sb("lnb", [128, 512])
    BS = sb("bs", [128, 512])
    WST = sb("wst", [128, 4, 128], BF16)
    PAR8 = sb("par8", [128, 3, 8])
    LG = sb("lg", [128, 8])
    TMP8 = [sb("tmp8_%d" % i, [128, 8]) for i in range(3)]
    MASKT = sb("maskt", [128, 4, 128])
    RBQF = sb("rbqf", [128, 4, 128], BF16)
    RBQB = sb("rbqb", [128, 4, 128], BF16)
    KD = sb("kd", [128, 8])
    CH = sb("ch", [128, 8])
    RNG = sb("rng", [128, 4])
    GNG = sb("gng", [128, 1])
    CW = sb("cw", [128, 12, 3])
    NAR = sb("nar", [128, 8])
    DTB = PAR8[:, 2, :]

    P.dma("act", FNG, V(fng_d.ap[0].partition_broadcast(128), fng_d.key))

    def load_layer_params(l):
        new_phase()
        WS32 = tmp("ws32", [128, 4, 128])
        CW3 = tmp("cw3", [128, 1536])[0:3, :]
        def bcast(dst, src_row):
            P.dma("act", dst, V(src_row.ap.partition_broadcast(128), src_row.key))
        bcast(NG, norm_g_d[l])
        bcast(LNG, gm_ln_g_d[l])
        bcast(LNB, gm_ln_b_d[l])
        bcast(BS, gm_b_s_d[l])
        bcast(PAR8[:, 0, :], ret_dl_d[l])
        bcast(PAR8[:, 1, :], a_log_d[l])
        bcast(PAR8[:, 2, :], dtb_d[l])
        P.dma("act", WS32, gm_w_s_d[l].re("g p q -> p g q"))
        P.dma("act", RNG, ret_ng_d[l].re("(h d) -> d h", h=4), allow_slow_non_contiguous=True)
        P.dma("act", GNG, gdn_ng_d[l].re("(d o) -> d o", o=1), allow_slow_non_contiguous=True)
        P.dma("act", CW3, conv_w_d[l])
        pt = ps([128, 4, 128])
        for g in range(4):
            P.tr(pt[:, g], WS32[:, g], IDF)
        P.copy("dve", WST, pt)
        pc = ps([128, 12, 3])
        for t in range(12):
            P.tr(pc[:, t], CW3[:, t * 128:(t + 1) * 128], IDF[0:3, 0:3])
        P.copy("dve", CW, pc)
        xdl = PAR8[:, 0, :]
        P.act(TMP8[0], xdl, AF.Abs)
        P.act(TMP8[0], TMP8[0], AF.Exp, scale=-1.0)
        P.act(TMP8[0], TMP8[0], AF.Ln, bias=1.0)
        P.ts("dve", TMP8[1], xdl, 0.0, ALU.min)
        P.tt("dve", LG, TMP8[1], TMP8[0], ALU.subtract)
        for h in range(4):
            e1 = tmp("mk_e1", [128, 128])
            P.ts("dve", e1, PM, LG[:, h:h + 1], ALU.mult)
            e2 = tmp("mk_e2", [128, 128])
            P.stt("dve", e2, QM, LG[:, 4 + h:5 + h], e1, ALU.mult, ALU.add)
            P.act(MASKT[:, h], e2, AF.Exp)
            P.act(RBQF[:, h], POSR[:, 0], AF.Exp, scale=LG[:, h:h + 1])
            P.act(RBQB[:, h], POSR[:, 2], AF.Exp, scale=LG[:, 4 + h:5 + h])
            P.act(KD[:, h:h + 1], POSC[:, 1:2], AF.Exp, scale=LG[:, h:h + 1])
            P.act(KD[:, 4 + h:5 + h], POSC[:, 3:4], AF.Exp, scale=LG[:, 4 + h:5 + h])
        P.act(CH, LG, AF.Exp, scale=128.0)
        P.act(NAR, PAR8[:, 1, :], AF.Exp)
        P.ts("dve", NAR, NAR, -1.0, ALU.mult)

    top_live = [0]

    def wregion(kind, nunits):
        if kind == "WBUF":
            return WBUF
        u0 = NU - nunits
        v = V(ARENA[:, u0 * 256:NU * 256], KS(ukeys[u0 * 4:NU * 4]))
        return v.bitcast(BF16)

    def load_w(src2d, ncols, c0, region, kcs=8, off=0):
        view = region[:, off:off + kcs * ncols].re("p (k n) -> p k n", k=kcs)
        src = src2d.re("(k p) n -> p k n", p=128)
        step = 2 if kcs % 2 == 0 else kcs
        for k0 in range(0, kcs, step):
            P.dma("pool", view[:, k0:k0 + step, :], src[:, k0:k0 + step, c0:c0 + ncols])
        return view

    WPLAN = {"gmlp": ("WBUF", 0), "ret": ("TOP", 32), "gdn": ("WBUF", 0),
             "m0": ("TOP", 24), "m1": ("WBUF", 0), "m2": ("TOP", 40)}

    conv_jobs = []
    WB16 = {}

    def add_conv(key, src2d):
        rows, cols = src2d.shape
        dst = dscr("wb_%s_%d" % key, [rows, cols], BF16)
        WB16[key] = dst
        conv_jobs.append((key, src2d, dst))

    for l_ in range(depth):
        add_conv(("gmlp", l_), w_in_d[l_][:, 0:1536])
        add_conv(("ret", l_), w_in_d[l_][:, 1536:3584])
        add_conv(("gdn", l_), w_in_d[l_][:, 3584:5648])
        for n_ in range(3):
            add_conv(("g%d" % n_, l_), w_gate_d[l_][:, n_ * 1024:(n_ + 1) * 1024])
            add_conv(("b%d" % n_, l_), w_bo_d[l_][n_ * 512:(n_ + 1) * 512, :])
            if n_ == 2:
                add_conv(("out", l_), w_out_d[l_])
    conv_pos = [0]

    def conv_upto(idx):
        while conv_pos[0] <= idx and conv_pos[0] < len(conv_jobs):
            key, src2d, dst = conv_jobs[conv_pos[0]]
            conv_pos[0] += 1
            sv = src2d.re("(k p) n -> p k n", p=128)
            dv = dst.re("(k p) n -> p k n", p=128)
            kcs = sv.shape[1]
            for k0 in range(0, kcs, 2):
                P.dma("pool", dv[:, k0:k0 + 2, :], sv[:, k0:k0 + 2, :])

    def load_wb(key, region, off=0):
        idx = [i for i, j in enumerate(conv_jobs) if j[0] == key][0]
        conv_upto(idx + 2)
        src = WB16[key]
        rows, cols = src.shape
        kcs = rows // 128
        view = region[:, off:off + kcs * cols].re("p (k n) -> p k n", k=kcs)
        sv = src.re("(k p) n -> p k n", p=128)
        for k0 in range(0, kcs, 2):
            P.dma("sp", view[:, k0:k0 + 2, :], sv[:, k0:k0 + 2, :])
        return view

    def issue_w(tag, l):
        kind, nu = WPLAN[tag]
        reg = wregion(kind, nu)
        if tag in ("gmlp", "ret", "gdn"):
            return load_wb((tag, l), reg)
        n = int(tag[1])
        WG = load_wb(("g%d" % n, l), reg)
        WB = load_wb(("b%d" % n, l), reg, off=8192)
        WO = load_wb(("out", l), reg, off=12288) if n == 2 else None
        return (WG, WB, WO)

    def rstd_from(dst, src, scale, eps):
        P.act(dst, src, AF.Ln, bias=eps, scale=scale)
        P.act(dst, dst, AF.Exp, scale=-0.5)

    def phase0(l, s):
        new_phase()
        src = x_d if l == 0 else xres_d
        for c in range(nch):
            xt = tmp("x32", [128, 1024], F32, 2)
            P.dma("sp", xt, src[s * nch + c])
            junk = tmp("junkb", [128, 1024], BF16, 2)
            ss = tmp("ss", [128, 1], F32, 2)
            P.act(junk, xt, AF.Square, accum=ss)
            rs = tmp("rs", [128, 1], F32, 2)
            rstd_from(rs, ss, 1.0 / D_MODEL, EPS)
            hb = tmp("hb", [128, 1024], BF16, 2)
            P.stt("dve", hb, xt, rs, NG, ALU.mult, ALU.mult)
            for half in range(2):
                pt = ps([128, 4, 128], BF16)
                for k in range(4):
                    kk = half * 4 + k
                    P.tr(pt[:, k], hb[:, kk * 128:(kk + 1) * 128], IDB)
                P.copy("act" if half == 0 else "dve",
                       hT[:, half * 4:(half + 1) * 4, 1 + c * 128:1 + (c + 1) * 128], pt)

    def tok(c):
        return slice(1 + c * 128, 1 + (c + 1) * 128)

    def proj_fm(W, col0, c, nt=4, g=None):
        p = ps([128, nt, 128], g=g)
        for t in range(nt):
            for kc in range(8):
                P.mm(p[:, t], W[:, kc, col0 + t * 128:col0 + (t + 1) * 128], hT[:, kc, tok(c)],
                     start=(kc == 0), stop=(kc == 7))
        return p

    def proj_tm(W, col0, c, n=512, g=None):
        p = ps([128, n], g=g)
        for kc in range(8):
            P.mm(p, hT[:, kc, tok(c)], W[:, kc, col0:col0 + n], start=(kc == 0), stop=(kc == 7))
        return p

    def phase_gmlp(l, s, W, pre):
        new_phase(32)
        pre()

        def gen(c, g):
            pu = proj_fm(W, 0, c, g=g)
            yield
            gu = tmp("gu", [128, 4, 128], BF16, 2)
            P.act(gu, pu, AF.Gelu)
            pz = proj_fm(W, 1024, c, g=g)
            yield
            sz = tmp("sz", [128, 4, 128], BF16, 2)
            P.act(sz, pz, AF.Silu)
            pv = proj_tm(W, 512, c, g=g)
            yield
            gv = tmp("gv", [128, 512], F32, 2)
            vsum = tmp("vsum", [128, 1], F32, 2)
            P.act(gv, pv, AF.Gelu, accum=vsum)
            ug = tmp("ug", [128, 4, 128], BF16, 2)
            P.tt("pool", ug, gu, sz, ALU.mult)
            yield
            nmean = tmp("nmean", [128, 1], F32, 2)
            P.ts("dve", nmean, vsum, -1.0 / 512, ALU.mult)
            yield
            cen = tmp("cen", [128, 512], F32, 2)
            P.ts("dve", cen, gv, nmean, ALU.add)
            yield
            junk = tmp("junkb", [128, 1024], BF16, 2)
            vss = tmp("vss", [128, 1], F32, 2)
            P.act(junk[:, 0:512], cen, AF.Square, accum=vss)
            yield
            vr = tmp("vr", [128, 1], F32, 2)
            P.act(vr, vss, AF.Ln, bias=EPS, scale=1.0 / 512)
            yield
            P.act(vr, vr, AF.Exp, scale=-0.5)
            yield
            t1 = tmp("gm_t1", [128, 512], F32, 2)
            P.stt("dve", t1, cen, vr, LNG, ALU.mult, ALU.mult)
            yield
            vn = tmp("vn", [128, 512], BF16, 2)
            P.tt("pool", vn, t1, LNB, ALU.add)
            yield
            pm = ps([128, 4, 128], g=g)
            for gg in range(4):
                P.mm(pm[:, gg], vn[:, gg * 128:(gg + 1) * 128], WST[:, gg])
            yield
            t2 = tmp("gm_t2", [128, 4, 128], F32, 2)
            P.tt("dve", t2, pm, BS.re("p (g q) -> p g q", g=4), ALU.add)
            yield
            ya = tmp("ya", [128, 4, 128], BF16, 2)
            P.tt("pool", ya, t2, ug, ALU.mult)
            P.dma("sp", Y_d[0][c], ya.re("p a b -> p (a b)"))
            yield

        run_window(gen, range(nch), 2, 6)

    def rotary(src32, c, name):
        q4 = src32.re("p (h t f) -> p h t f", h=4, t=2)
        cosb = COS[:, c, :].bc([128, 4, 2, 64], axes=(1, 1))
        t1 = tmp("rot_t1", [128, 4, 2, 64])
        P.tt("pool", t1, q4, cosb, ALU.mult)
        t2 = tmp("rot_t2", [128, 4, 2, 64])
        P.tt("dve", t2[:, :, 0, :], q4[:, :, 1, :], NSIN[:, c, :].bc([128, 4, 64], axes=(1,)), ALU.mult)
        P.tt("dve", t2[:, :, 1, :], q4[:, :, 0, :], SIN[:, c, :].bc([128, 4, 64], axes=(1,)), ALU.mult)
        qr = tmp(name, [128, 512], BF16)
        P.tt("pool", qr.re("p (h t f) -> p h t f", h=4, t=2), t1, t2, ALU.add)
        return qr

    def transpose4(src, name, eng="act", g=None):
        pt = ps([128, 4, 128], BF16, g=g)
        for h in range(4):
            blk = src[:, h] if len(src.shape) == 3 else src[:, h * 128:(h + 1) * 128]
            P.tr(pt[:, h], blk, IDB)
        if name is None:
            return pt
        o = tmp(name, [128, 4, 128], BF16)
        P.copy(eng, o, pt)
        return o

    def state_update(Sx, Sb, lhs, rhs, dec_bc, g=None):
        pS = ps([128, 4, 128], g=g)
        for h in range(4):
            P.mm(pS[:, h], lhs[:, h], rhs[:, h])
        P.tt("pool", Sx, Sx, dec_bc, ALU.mult)
        P.tt("dve", Sx, Sx, pS, ALU.add)
        P.copy("act", Sb, Sx)

    SF = sb("SF", [128, 4, 128])
    SFB = sb("SFB", [128, 4, 128], BF16)
    SFB2 = [SFB, sb("SFB2", [128, 4, 128], BF16)]
    SFL2 = [sb("SFL", [128, 4, 128], BF16), sb("SFL2", [128, 4, 128], BF16)] if PREC & 2 else [None, None]

    def rotary_gen(src32, c, name):
        q4 = src32.re("p (h t f) -> p h t f", h=4, t=2)
        cosb = COS[:, c, :].bc([128, 4, 2, 64], axes=(1, 1))
        t1 = tmp("rot_t1" + name, [128, 4, 2, 64], F32, 2)
        P.tt("pool" if name == "q" else "dve", t1, q4, cosb, ALU.mult)
        t2 = tmp("rot_t2" + name, [128, 4, 2, 64], F32, 2)
        P.tt("dve", t2[:, :, 0, :], q4[:, :, 1, :], NSIN[:, c, :].bc([128, 4, 64], axes=(1,)), ALU.mult)
        P.tt("dve", t2[:, :, 1, :], q4[:, :, 0, :], SIN[:, c, :].bc([128, 4, 64], axes=(1,)), ALU.mult)
        return t1, t2

    def phase_ret_fwd(l, s, W, pre):
        new_phase(32)
        pre()
        P.memset("pool", SF, 0.0)
        P.memset("pool", SFB2[0], 0.0)
        if PREC & 2:
            P.memset("pool", SFL2[0], 0.0)

        def gen(c, g):
            S_old, S_new = SFB2[c % 2], SFB2[(c + 1) % 2]
            L_old, L_new = SFL2[c % 2], SFL2[(c + 1) % 2]
            pq = proj_tm(W, 0, c, g=g)
            pk = proj_tm(W, 512, c, g=g)
            yield
            q32 = tmp("q32", [128, 512], F32, 2)
            P.copy("act", q32, pq)
            k32 = tmp("k32", [128, 512], F32, 2)
            P.act(k32, pk, AF.Copy, scale=float(128 ** -0.5))
            pv = proj_tm(W, 1024, c, g=g)
            pz = proj_fm(W, 1536, c, g=g)
            yield
            vb = tmp("r_vb", [128, 4, 128], BF16, 2)
            P.copy("dve", vb.re("p a b -> p (a b)"), pv)
            szr = tmp("r_sz", [128, 4, 128], BF16, 2)
            P.act(szr, pz, AF.Silu)
            P.dma("sp", RZ_d[c], szr.re("p a b -> p (a b)"))
            P.dma("sp", RV_d[c], vb.re("p a b -> p (a b)"))
            qa, qb = rotary_gen(q32, c, "q")
            yield
            ka, kb = rotary_gen(k32, c, "k")
            yield
            qr = tmp("r_qr", [128, 512], BF16, 2)
            P.tt("pool", qr.re("p (h t f) -> p h t f", h=4, t=2), qa, qb, ALU.add)
            kr = tmp("r_kr", [128, 512], BF16, 2)
            P.tt("dve", kr.re("p (h t f) -> p h t f", h=4, t=2), ka, kb, ALU.add)
            yield
            ptq = transpose4(qr, None, g=g)
            ptk = transpose4(kr, None, g=g)
            kr4 = kr.re("p (h d) -> p h d", h=4)
            kdf = tmp("r_kdf", [128, 4, 128], BF16, 2)
            P.tt("dve", kdf, kr4, KD[:, 0:4].bc([128, 4, 128], axes=(2,)), ALU.mult)
            kdb = tmp("r_kdb", [128, 4, 128], BF16, 2)
            P.tt("pool", kdb, kr4, KD[:, 4:8].bc([128, 4, 128], axes=(2,)), ALU.mult)
            P.dma("sp", RKDB_d[c], kdb.re("p a b -> p (a b)"))
            yield
            qT = tmp("r_qT", [128, 4, 128], BF16, 2)
            P.copy("act", qT, ptq)
            kT = tmp("r_kT", [128, 4, 128], BF16, 2)
            P.copy("dve", kT, ptk)
            yield
            psc = ps([128, 4, 128], g=g)
            for h in range(4):
                P.mm(psc[:, h], kT[:, h], qT[:, h])
            qdf = tmp("r_qdf", [128, 4, 128], BF16, 2)
            P.tt("pool", qdf, qT, RBQF, ALU.mult)
            qdb = tmp("r_qdb", [128, 4, 128], BF16, 2)
            P.tt("pool", qdb, qT, RBQB, ALU.mult)
            P.dma("sp", RQDB_d[c], qdb.re("p a b -> p (a b)"))
            yield
            pT = tmp("r_pT", [128, 4, 128], BF16, 2)
            P.tt("dve", pT, psc, MASKT, ALU.mult)
            yield
            po = ps([128, 4, 128], g=g)
            for h in range(4):
                P.mm(po[:, h], vb[:, h], pT[:, h], start=True, stop=False)
                if PREC & 2:
                    P.mm(po[:, h], L_old[:, h], qdf[:, h], start=False, stop=False)
                P.mm(po[:, h], S_old[:, h], qdf[:, h], start=False, stop=True)
            pS = ps([128, 4, 128], g=g)
            for h in range(4):
                P.mm(pS[:, h], kdf[:, h], vb[:, h])
            P.tt("pool", SF, SF, CH[:, 0:4].bc([128, 4, 128], axes=(2,)), ALU.mult)
            yield
            P.tt("dve", SF, SF, pS, ALU.add)
            of32 = tmp("r_of", [128, 4, 128], F32, 2)
            P.copy("act", of32, po)
            P.dma("sp", ROF_d[c], of32.re("p a b -> p (a b)"))
            yield
            P.copy("act", S_new, SF)
            yield
            if PREC & 2:
                P.tt("pool", L_new, SF, S_new, ALU.subtract)
                yield

        run_window(gen, range(nch), 2, 6)

    def load_t(q, name, shape, dt, src, n=2):
        t = tmp(name, shape, dt, n)
        if len(shape) == 3:
            P.dma(q, t.re("p a b -> p (a b)"), src)
        elif len(shape) == 4:
            P.dma(q, t.re("p a b c -> p (a b c)"), src)
        else:
            P.dma(q, t, src)
        return t

    def ret_bwd_gen(l, s, g):
        P.memset("pool", SF, 0.0)
        P.memset("pool", SFB, 0.0)
        if PREC & 2:
            P.memset("pool", SFL2[0], 0.0)
        def rloads(c):
            return (load_t("sp", "b_qdb", [128, 4, 128], BF16, RQDB_d[c]),
                    load_t("sp", "b_kdb", [128, 4, 128], BF16, RKDB_d[c]),
                    load_t("sp", "b_v", [128, 4, 128], BF16, RV_d[c]),
                    load_t("sp", "b_of", [128, 4, 128], F32, ROF_d[c]),
                    load_t("sp", "b_sz", [128, 4, 128], BF16, RZ_d[c]))

        nxt_l = rloads(nch - 1)
        for c in reversed(range(nch)):
            qdb, kdb, vb, of32, szr = nxt_l
            if c > 0:
                nxt_l = rloads(c - 1)
            pob = ps([128, 4, 128], g=g)
            for h in range(4):
                if PREC & 2:
                    P.mm(pob[:, h], SFB[:, h], qdb[:, h], start=True, stop=False)
                    P.mm(pob[:, h], SFL2[0][:, h], qdb[:, h], start=False, stop=True)
                else:
                    P.mm(pob[:, h], SFB[:, h], qdb[:, h])
            pS = ps([128, 4, 128], g=g)
            for h in range(4):
                P.mm(pS[:, h], kdb[:, h], vb[:, h])
            yield
            o = tmp("b_o", [128, 512])
            P.tt("dve", o, pob.re("p a b -> p (a b)"), of32.re("p a b -> p (a b)"), ALU.add)
            P.tt("pool", SF, SF, CH[:, 4:8].bc([128, 4, 128], axes=(2,)), ALU.mult)
            yield
            P.tt("dve", SF, SF, pS, ALU.add)
            osq = tmp("b_osq", [128, 512])
            P.act(osq, o, AF.Square)
            yield
            P.copy("act", SFB, SF)
            if PREC & 2:
                P.tt("pool", SFL2[0], SF, SFB, ALU.subtract)
            pmean = ps([128, 512], g=g)
            P.mm(pmean, ONESDIV, o)
            pmsq = ps([128, 512], g=g)
            P.mm(pmsq, ONESDIV, osq)
            yield
            cen = tmp("b_cen", [128, 512])
            P.stt("dve", cen, pmean, -1.0, o, ALU.mult, ALU.add)
            yield
            m2 = tmp("b_m2", [128, 512])
            P.act(m2, pmean, AF.Square)
            yield
            var = tmp("b_var", [128, 512])
            P.tt("dve", var, pmsq, m2, ALU.subtract)
            yield
            rstd_from(var, var, 1.0, EPS)
            yield
            t = tmp("b_t", [128, 4, 128])
            P.tt("pool", t.re("p a b -> p (a b)"), cen, var, ALU.mult)
            yield
            yb = tmp("b_yb", [128, 4, 128], BF16, 2)
            for h in range(4):
                P.stt("dve", yb.h(h), t.h(h), RNG[:, h:h + 1], szr.h(h), ALU.mult, ALU.mult)
            P.dma("sp", Y_d[1][c], yb.re("p a b -> p (a b)"))
            yield

    GS = heads(sb("GS", [128, 4, 128]), 4)
    GSB = sb("GSB", [128, 4, 128], BF16)
    GSL = sb("GSL", [128, 4, 128], BF16) if PREC & 1 else None
    GPB_d = dscr("gpb", [nch, 128, 512], BF16)
    GSTB_d = dscr("gstb", [nch, 128, 512], BF16)
    GQGB_d = dscr("gqgb", [nch, 128, 512], BF16)

    def gdn_recur(d, KQ, vtok, kd, SC, Pm, ST, qg, res, g):
        o = 4 * d
        one = len(g["banks"]) == 1
        pks = ps([128, 4, 128], g=g)
        for h in range(4):
            if PREC & 1:
                P.mm(pks[:, h], KQ[:, h, 0, :], GSB[:, h], start=True, stop=False)
                P.mm(pks[:, h], KQ[:, h, 0, :], GSL[:, h], start=False, stop=True)
            else:
                P.mm(pks[:, h], KQ[:, h, 0, :], GSB[:, h])
        yield
        r = tmp("g_r%d" % d, [128, 4, 128], BF16)
        for h in range(4):
            P.stt("dve", r.h(h), pks[:, h], SC[:, 16 + o + h:17 + o + h], vtok.h(h), ALU.mult, ALU.add)
        yield
        pvn = ps([128, 4, 128], g=g)
        for h in range(4):
            P.mm(pvn[:, h], Pm.h(h), r.h(h))
        yield
        vnew = tmp("g_vnew%d" % d, [128, 4, 128], BF16)
        for h in range(4):
            P.act(vnew.h(h), pvn[:, h], AF.Copy, scale=SC[:, 40 + o + h:41 + o + h])
        yield
        of32 = tmp("g_of%d" % d, [128, 4, 128], F32, 2)
        if one:
            po = ps([128, 4, 128], g=g)
            for h in range(4):
                P.mm(po[:, h], vnew.h(h), ST.h(h), start=True, stop=False)
                if PREC & 1:
                    P.mm(po[:, h], GSL[:, h], qg[:, h], start=False, stop=False)
                P.mm(po[:, h], GSB[:, h], qg[:, h], start=False, stop=True)
            yield
            P.copy("act", of32, po)
            pS = ps([128, 4, 128], g=g)
            for h in range(4):
                P.mm(pS[:, h], kd.h(h), vnew.h(h))
            yield
        else:
            pS = ps([128, 4, 128], g=g)
            for h in range(4):
                P.mm(pS[:, h], kd.h(h), vnew.h(h))
            po = ps([128, 4, 128], g=g)
            for h in range(4):
                P.mm(po[:, h], vnew.h(h), ST.h(h), start=True, stop=False)
                if PREC & 1:
                    P.mm(po[:, h], GSL[:, h], qg[:, h], start=False, stop=False)
                P.mm(po[:, h], GSB[:, h], qg[:, h], start=False, stop=True)
            yield
        for h in range(4):
            P.stt("dve", GS.h(h), GS.h(h), SC[:, 48 + o + h:49 + o + h], pS[:, h], ALU.mult, ALU.add)
        if not one:
            P.copy("act", of32, po)
        yield
        P.copy("act", GSB, GS)
        res["o"] = of32
        yield
        if PREC & 1:
            P.tt("pool", GSL, GS, GSB, ALU.subtract)
            yield

    F32R = mybir.dt.float32r

    def gdn_chain(d, c, I, g, R):
        o = 4 * d
        sx = str(d)
        KQ, SC = I["KQ"], I["SC"]

        def r_(v):
            return v

        dg = tmp("g_dg" + sx, [128, 4, 128])
        P.tt("pool", dg, IDF.bc([128, 4, 128], axes=(1,)), SC[:, o:o + 4].bc([128, 4, 128], axes=(2,)), ALU.mult)
        yield
        prb = ps([128, 4, 128], g=g)
        P.mm(prb.re("p a b -> p (a b)"), ONESF, dg.re("p a b -> p (a b)"))
        yield
        E = tmp("g_E" + sx, [128, 4, 128])
        for h in range(4):
            P.stt("dve", E.h(h), prb[:, h], SC[:, o + h:o + h + 1], NEGM[d], ALU.subtract, ALU.add)
        yield
        P.act(E, E, AF.Exp)
        D = E
        P.act(dg, prb, AF.Exp)
        EG = dg
        pg = [ps([128, 2, 256], g=g), ps([128, 2, 256], g=g)]
        for h in range(4):
            P.mm(pg[h // 2][:, h % 2], KQ[:, h, 0, :], KQ[:, h].re("p a b -> p (a b)"))
        yield
        Ds = tmp("g_Ds" + sx, [128, 4, 128])
        P.tt("pool", Ds, D, OFFD.bc([128, 4, 128], axes=(1,)), ALU.mult)
        nq = 2 if d == 0 else 1
        qg = tmp("g_qg" + sx, [128, 4, 128], BF16, nq)
        P.tt("pool", qg, KQ[:, :, 1, :], EG, ALU.mult)
        ST = tmp("g_ST" + sx, [128, 4, 128], BF16, nq)
        for hh in range(2):
            P.tt("dve", ST[:, 2 * hh:2 * hh + 2], pg[hh][:, :, 128:256], D[:, 2 * hh:2 * hh + 2], ALU.mult)
        yield
        Nc = tmp("g_N0" + sx, [128, 4, 128], F32)
        for h in range(4):
            P.stt("dve", r_(Nc.h(h)), pg[h // 2][:, h % 2, 0:128], SC[:, 32 + o + h:33 + o + h], Ds.h(h),
                  ALU.mult, ALU.mult)
        yield
        ptm = ps([128, 4, 128], F32, g=g)
        for h in range(4):
            P.tr(ptm[:, h], Nc.h(h), IDF)
        Pm = tmp("g_P" + sx, [128, 4, 128], F32, 2)
        P.tt("pool", r_(Pm), Nc, IDF.bc([128, 4, 128], axes=(1,)), ALU.add)
        yield
        Mc = tmp("g_M0" + sx, [128, 4, 128], F32)
        P.copy("act", r_(Mc), ptm)
        yield
        n2 = tmp("g_N2" + sx, [128, 4, 128], F32)
        mring = [dg, E]
        nring = [Ds, n2]

        LSW = CHAIN_LSW

        def asdt(v, lv):
            if lv < LSW - 1:
                return v
            return v.re("p a b -> p (a b)").bitcast(BF16)[:, 0:512].re("p (a b) -> p a b", a=4)

        pp = None
        for lv in range(1, 7):
            pm_ = ps([128, 4, 128], g=g)
            for h in range(4):
                P.mm(pm_[:, h], Nc[:, h], Mc[:, h])
            if lv < 6:
                pn_ = ps([128, 4, 128], g=g)
                for h in range(4):
                    P.mm(pn_[:, h], Mc[:, h], Nc[:, h])
            yield
            Mn = asdt(mring[lv % 2], lv)
            P.copy("act", Mn, pm_)
            if lv < 6:
                Nn = asdt(nring[lv % 2], lv)
                P.copy("act" if lv % 2 == 1 else "dve", Nn, pn_)
                Nc = Nn
            Mc = Mn
            if pp is not None:
                Pn = asdt(tmp("g_P" + sx, [128, 4, 128], F32, 2), lv)
                P.tt("dve", Pn, pp, Pm, ALU.add)
                Pm = Pn
            elif lv >= LSW - 1:
                Pn = asdt(tmp("g_P" + sx, [128, 4, 128], F32, 2), lv)
                P.copy("dve", Pn, Pm)
                Pm = Pn
            yield
            pp = ps([128, 4, 128], g=g)
            for h in range(4):
                P.mm(pp[:, h], Mc[:, h], Pm[:, h])
        yield
        if d == 0:
            Pf = tmp("g_Pfin", [128, 4, 128], BF16, 2)
            P.tt("dve", Pf, pp, Pm, ALU.add)
            R.update(Pm=Pf, ST=ST, qg=qg, KQ=KQ, SC=SC, vtok=I["vtok"], kdf=I["kdf"], c=c)
        else:
            Pf = tmp("g_Pfin1", [128, 4, 128], BF16)
            P.tt("dve", Pf, pp, Pm, ALU.add)
            P.dma("sp", GPB_d[c], Pf.re("p a b -> p (a b)"))
            P.dma("sp", GSTB_d[c], ST.re("p a b -> p (a b)"))
            P.dma("sp", GQGB_d[c], qg.re("p a b -> p (a b)"))
        yield

    def gdn_fwd_recur(R, g):
        res = {}
        for _ in gdn_recur(0, R["KQ"], R["vtok"], R["kdf"], R["SC"], R["Pm"], R["ST"], R["qg"], res, g):
            yield
        P.dma("sp", GOF_d[R["c"]], res["o"].re("p a b -> p (a b)"))
        yield

    def gdn_common(c, W, I, g):
        X = tmp("g_X", [128, 12, 128])
        cs = {}

        def conv_mm(b):
            pc = ps([128, 3, 132], g=g)
            for j in range(3):
                t = 3 * b + j
                for kc in range(8):
                    P.mm(pc[:, j, 0:130], W[:, kc, t * 128:(t + 1) * 128],
                         hT[:, kc, c * 128:c * 128 + 130], start=(kc == 0), stop=(kc == 7))
            cs[b] = [pc]

        def conv_dve(b):
            pc = cs[b][0]
            c1 = tmp("g_c1", [128, 3, 128], F32, 2)
            c2 = tmp("g_c2", [128, 3, 128])
            c3 = tmp("g_c3", [128, 3, 128])
            cwb = CW[:, 3 * b:3 * b + 3, :]
            P.tt("dve", c1, pc[:, :, 0:128], cwb[:, :, 0:1].bc([128, 3, 128]), ALU.mult)
            P.tt("dve", c2, pc[:, :, 1:129], cwb[:, :, 1:2].bc([128, 3, 128]), ALU.mult)
            P.tt("dve", c3, pc[:, :, 2:130], cwb[:, :, 2:3].bc([128, 3, 128]), ALU.mult)
            cs[b] = [c1, c2, c3]

        def conv_pool(b):
            c1, c2, c3 = cs[b]
            P.tt("pool", c1, c1, c2, ALU.add)
            P.tt("pool", c1, c1, c3, ALU.add)

        def conv_act(b):
            P.act(X[:, 3 * b:3 * b + 3, :], cs[b][0], AF.Silu)

        conv_mm(0)
        yield
        conv_dve(0)
        yield
        for b in range(1, 4):
            conv_mm(b)
            conv_pool(b - 1)
            yield
            conv_dve(b)
            conv_act(b - 1)
            yield
        conv_pool(3)
        pab = ps([128, 16], g=g)
        for kc in range(8):
            P.mm(pab, hT[:, kc, tok(c)], W[:, kc, 2048:2064], start=(kc == 0), stop=(kc == 7))
        yield
        conv_act(3)
        ab = tmp("g_ab", [128, 16])
        P.copy("dve", ab, pab)
        sq = tmp("g_sq", [128, 8, 128], BF16)
        P.act(sq[:, 0:4], X[:, 0:4], AF.Square, scale=float(128 ** 0.5))
        P.act(sq[:, 4:8], X[:, 4:8], AF.Square)
        yield
        xa = tmp("g_xa", [128, 8])
        P.tt("dve", xa[:, 0:4], ab[:, 0:4], DTB[:, 0:4], ALU.add)
        P.tt("dve", xa[:, 4:8], ab[:, 8:12], DTB[:, 4:8], ALU.add)
        SC = tmp("g_SC", [128, 56], F32, 3)
        P.act(SC[:, 40:44], ab[:, 4:8], AF.Sigmoid)
        P.act(SC[:, 44:48], ab[:, 12:16], AF.Sigmoid)
        pn1 = ps([128, 512], g=g)
        P.mm(pn1, ONESB, sq[:, 4:8].re("p a b -> p (a b)"))
        yield
        rn = tmp("g_rn", [128, 8, 128])
        P.act(rn[:, 4:8].re("p a b -> p (a b)"), pn1, AF.Ln, bias=EPS)
        ax = tmp("g_ax", [128, 8])
        P.act(ax, xa, AF.Abs)
        rx = tmp("g_rx", [128, 8])
        P.ts("dve", rx, xa, 0.0, ALU.max)
        P.ts("dve", SC[:, 32:40], SC[:, 40:48], -1.0, ALU.mult)
        pn2 = ps([128, 512], g=g)
        P.mm(pn2, ONESB, sq[:, 0:4].re("p a b -> p (a b)"))
        yield
        P.act(rn[:, 0:4].re("p a b -> p (a b)"), pn2, AF.Ln, bias=128 * EPS)
        P.act(ax, ax, AF.Exp, scale=-1.0)
        yield
        P.act(rn, rn, AF.Exp, scale=-0.5)
        P.act(ax, ax, AF.Ln, bias=1.0)
        yield
        KQ = tmp("g_KQ", [128, 4, 2, 128], BF16, 3)
        P.tt("pool", KQ[:, :, 0, :], X[:, 4:8], rn[:, 4:8], ALU.mult)
        P.tt("pool", KQ[:, :, 1, :], X[:, 0:4], rn[:, 0:4], ALU.mult)
        vTb = tmp("g_vTb", [128, 4, 128], BF16)
        P.copy("pool", vTb, X[:, 8:12])
        P.tt("dve", rx, rx, ax, ALU.add)
        yield
        gg = tmp("g_g", [128, 8])
        P.tt("dve", gg, rx, NAR, ALU.mult)
        yield
        pgc = ps([128, 16], g=g)
        P.mm(pgc[:, 0:4], TRIF, gg[:, 0:4])
        P.mm(pgc[:, 4:8], TRIB, gg[:, 4:8])
        P.mm(pgc[:, 8:16], ONESF, gg)
        yield
        P.copy("dve", SC[:, 0:16], pgc)
        ptv = transpose4(vTb, None, g=g)
        yield
        vtok = tmp("g_vtok", [128, 4, 128], BF16, 3)
        P.copy("act", vtok, ptv)
        ptk = transpose4(KQ[:, :, 0, :], None, g=g)
        P.act(SC[:, 16:24], SC[:, 0:8], AF.Exp)
        dd = tmp("g_dd", [128, 8])
        P.tt("dve", dd, SC[:, 8:16], SC[:, 0:8], ALU.subtract)
        yield
        P.act(SC[:, 24:32], dd, AF.Exp)
        P.act(SC[:, 48:56], SC[:, 8:16], AF.Exp)
        P.ts("dve", SC[:, 16:24], SC[:, 16:24], -1.0, ALU.mult)
        yield
        kdf = tmp("g_kdf", [128, 4, 128], BF16, 3)
        P.tt("dve", kdf, ptk, SC[:, 24:28].bc([128, 4, 128], axes=(2,)), ALU.mult)
        kdb = tmp("g_kdb", [128, 4, 128], BF16)
        P.tt("dve", kdb, ptk, SC[:, 28:32].bc([128, 4, 128], axes=(2,)), ALU.mult)
        yield
        P.dma("sp", GKQ_d[c], KQ.re("p a b c -> p (a b c)"))
        P.dma("sp", GVT_d[c], vtok.re("p a b -> p (a b)"))
        P.dma("sp", GKDB_d[c], kdb.re("p a b -> p (a b)"))
        P.dma("sp", GSC_d[c], SC)
        I.update(KQ=KQ, SC=SC, vtok=vtok, kdf=kdf)
        yield

    def phase_gdn_fwd(l, s, W, pre):
        new_phase(0)
        pre()
        P.memset("pool", GS, 0.0)
        P.memset("pool", GSB, 0.0)
        if PREC & 1:
            P.memset("pool", GSL, 0.0)
        g0, g1, g2, g3 = bgroup([0, 1, 2]), bgroup([3, 4, 5]), bgroup([6]), bgroup([7])
        cur = {}
        run_gens([gdn_common(0, W, cur, bgroup([6, 7]))])
        prev = None
        for c in range(nch):
            nxt = {}
            R = {}
            gens = [gdn_chain(0, c, cur, g0, R), gdn_chain(1, c, cur, g1, None)]
            if prev is not None:
                gens.append(gdn_fwd_recur(prev, g3))
            if c + 1 < nch:
                gens.append(gdn_common(c + 1, W, nxt, g2))
            run_gens(gens)
            cur = nxt
            prev = R
        run_gens([gdn_fwd_recur(prev, g3)])

    def gdn_bwd_gen(l, s, g, Q):
        P.memset("pool", GS, 0.0)
        P.memset("pool", GSB, 0.0)
        if PREC & 1:
            P.memset("pool", GSL, 0.0)
        def gloads(c):
            return (load_t("sp", "gb_KQ", [128, 4, 2, 128], BF16, GKQ_d[c]),
                    load_t("sp", "gb_vtok", [128, 4, 128], BF16, GVT_d[c]),
                    load_t("sp", "gb_kdb", [128, 4, 128], BF16, GKDB_d[c]),
                    load_t("sp", "gb_SC", [128, 56], F32, GSC_d[c]),
                    load_t("sp", "gb_P", [128, 4, 128], BF16, GPB_d[c]),
                    load_t("sp", "gb_ST", [128, 4, 128], BF16, GSTB_d[c]),
                    load_t("sp", "gb_qg", [128, 4, 128], BF16, GQGB_d[c]),
                    load_t("sp", "gb_of", [128, 4, 128], F32, GOF_d[c], 3))

        nxt_l = gloads(nch - 1)
        for c in reversed(range(nch)):
            KQ, vtok, kdb, SC, Pm, ST, qg, of32 = nxt_l
            if c > 0:
                nxt_l = gloads(c - 1)
            res = {}
            for _ in gdn_recur(1, KQ, vtok, kdb, SC, Pm, ST, qg, res, g):
                yield
            assert len(Q) <= 1, "bwd norm generator fell behind the recurrence"
            Q.append((c, res["o"], of32))

    def gdn_bwd_norm(l, s, g, Q, Wg):
        done = 0
        while done < nch:
            if not Q:
                yield
                continue
            c, ob, of32 = Q.pop(0)
            done += 1
            o = tmp("gb_o", [128, 512])
            P.tt("pool", o, ob.re("p a b -> p (a b)"), of32.re("p a b -> p (a b)"), ALU.add)
            yield
            osq = tmp("gb_osq", [128, 512])
            P.act(osq, o, AF.Square)
            pz = proj_fm(Wg, 1536, c, g=g)
            yield
            szg = tmp("gb_sz", [128, 4, 128], BF16, 2)
            P.act(szg, pz, AF.Silu)
            pms = ps([128, 512], g=g)
            P.mm(pms, ONESDIV, osq)
            yield
            rs = osq
            P.act(rs, pms, AF.Ln, bias=EPS)
            yield
            P.act(rs, rs, AF.Exp, scale=-0.5)
            yield
            P.tt("pool", o, o, rs, ALU.mult)
            yield
            yc = tmp("gb_yc", [128, 512], BF16, 2)
            P.stt("dve", yc, o, GNG[:, 0:1], szg.re("p a b -> p (a b)"), ALU.mult, ALU.mult)
            P.dma("sp", Y_d[2][c], yc)
            yield

    def phase_bwd(l, s, pre, Wg):
        new_phase(24)
        pre()
        Q = []
        run_gens([gdn_bwd_gen(l, s, bgroup([4, 5]), Q), ret_bwd_gen(l, s, bgroup([0, 1, 2, 3])), gdn_bwd_norm(l, s, bgroup([6, 7]), Q, Wg)])

    def phase_merge(l, s, last, Wm, pres):
        new_phase(40)
        items = [(n, c) for n in range(3) for c in range(nch)]
        pl = {}
        Wsel = {}

        stored = set()

        def issue(i, force=False):
            if i >= len(items) or i in pl:
                return
            n, c = items[i]
            if n > 0 and (n - 1, c) not in stored:
                assert not force, "accumulator load issued before the producing store was recorded"
                return
            y = load_t("sp", "m_y", [128, 4, 128], BF16, Y_d[n][c], 4)
            acc = tmp("m_acc", [128, 8, 128], F32, 4)
            if n > 0:
                P.dma("sp", acc.re("p a b -> p (a b)"), ACC_d[c])
            pl[i] = (y, acc)

        issue(0)
        issue(1)

        if True:
            def gen(i, g):
                n, c = items[i]
                if c == 0:
                    Wsel[n] = Wm[n]()
                    pres[n]()
                WG, WB, WO = Wsel[n]
                issue(i, force=True)
                y, acc = pl.pop(i)
                issue(i + 2)
                mT = tmp("m_mT", [128, 8, 128], BF16, 2) if n == 2 else None
                if n == 2:
                    xt = tmp("x32", [128, 1024], F32, 2)
                    src = x_d if l == 0 else xres_d
                    P.dma("sp", xt, src[s * nch + c])
                for mh in range(2):
                    pp = ps([128, 4, 128], g=g)
                    for mt in range(4):
                        m = mh * 4 + mt
                        for kc in range(4):
                            P.mm(pp[:, mt], WB[:, kc, m * 128:(m + 1) * 128], y[:, kc],
                                 start=(kc == 0), stop=(kc == 3))
                    pgt = ps([128, 4, 128], g=g)
                    for mt in range(4):
                        m = mh * 4 + mt
                        for kc in range(8):
                            P.mm(pgt[:, mt], WG[:, kc, m * 128:(m + 1) * 128], hT[:, kc, tok(c)],
                                 start=(kc == 0), stop=(kc == 7))
                    yield
                    gs = tmp("m_gs", [128, 4, 128], F32, 2)
                    P.act(gs, pgt, AF.Sigmoid)
                    yield
                    asl = acc[:, mh * 4:(mh + 1) * 4]
                    if n == 0:
                        P.tt("dve", asl, pp, gs, ALU.mult)
                    else:
                        t = tmp("m_t", [128, 4, 128], F32, 2)
                        P.tt("dve", t, pp, gs, ALU.mult)
                        yield
                        if n == 1:
                            P.tt("pool", asl, asl, t, ALU.add)
                        else:
                            P.tt("pool", mT[:, mh * 4:(mh + 1) * 4], asl, t, ALU.add)
                    yield
                if n < 2:
                    P.dma("sp", ACC_d[c], acc.re("p a b -> p (a b)"))
                    stored.add((n, c))
                    yield
                    return
                xn = xt
                pos = []
                for nh in range(2):
                    po = ps([128, 512], g=g)
                    for kc in range(8):
                        P.mm(po, mT[:, kc], WO[:, kc, nh * 512:(nh + 1) * 512],
                             start=(kc == 0), stop=(kc == 7))
                    pos.append(po)
                yield
                for nh in range(2):
                    P.tt("dve", xn[:, nh * 512:(nh + 1) * 512], pos[nh], xt[:, nh * 512:(nh + 1) * 512], ALU.add)
                yield
                if not last:
                    P.dma("sp", xres_d[s * nch + c], xn)
                    yield
                    return
                junk = tmp("m_junk", [128, 1024], BF16, 1)
                ss = tmp("ss", [128, 1], F32, 2)
                P.act(junk, xn, AF.Square, accum=ss)
                yield
                rs = tmp("rs", [128, 1], F32, 2)
                P.act(rs, ss, AF.Ln, bias=EPS, scale=1.0 / D_MODEL)
                yield
                P.act(rs, rs, AF.Exp, scale=-0.5)
                yield
                P.stt("dve", xn, xn, rs, FNG, ALU.mult, ALU.mult)
                P.dma("sp", out_d[s * nch + c], xn)
                yield

            run_window(gen, range(len(items)), 2, 3)

    passes = [(l, s) for l in range(depth) for s in range(nseq)]
    Wn = {"gmlp": issue_w("gmlp", 0)}
    for pi, (l, s) in enumerate(passes):
        if s == 0:
            load_layer_params(l)
        last = (l == depth - 1)
        nxt = passes[pi + 1] if pi + 1 < len(passes) else None

        def pf(tag, ll=l):
            def f():
                Wn[tag] = issue_w(tag, ll)
            return f

        def pf_next():
            if nxt is not None:
                Wn["gmlp"] = issue_w("gmlp", nxt[0])

        phase0(l, s)
        phase_gmlp(l, s, Wn["gmlp"], pf("ret"))
        phase_ret_fwd(l, s, Wn["ret"], pf("gdn"))
        phase_gdn_fwd(l, s, Wn["gdn"], lambda: None)
        phase_bwd(l, s, pf("m0"), Wn["gdn"])
        phase_merge(l, s, last, [lambda: Wn["m0"], lambda: Wn["m1"], lambda: Wn["m2"]],
                    [pf("m1"), pf("m2"), pf_next])

    P.emit(st)
    st.close()
    return nc, P


_PARAM_NAMES = ["norm_g", "w_in", "gm_ln_g", "gm_ln_b", "gm_w_s", "gm_b_s", "ret_decay_logit",
                "ret_norm_g", "gdn_conv_w", "gdn_a_log", "gdn_dt_bias", "gdn_norm_g", "w_gate",
                "w_branch_out", "w_out", "final_norm_g"]


def prep_params(inputs, depth):
    f = lambda a: np.ascontiguousarray(np.asarray(a, dtype=np.float32))
    p = {}
    p["norm_g"] = f(inputs["norm_g"])
    p["w_in"] = f(inputs["w_in"])
    p["gm_ln_g"] = f(inputs["gm_ln_g"])
    p["gm_ln_b"] = f(inputs["gm_ln_b"])
    p["gm_w_s"] = f(inputs["gm_w_s"])
    p["gm_b_s"] = f(inputs["gm_b_s"]).reshape(depth, 512)
    p["ret_decay_logit"] = f(inputs["ret_decay_logit"]).reshape(depth, 8)
    p["ret_norm_g"] = f(inputs["ret_norm_g"])
    p["gdn_conv_w"] = f(inputs["gdn_conv_w"])
    p["gdn_a_log"] = f(inputs["gdn_a_log"]).reshape(depth, 8)
    p["gdn_dt_bias"] = f(inputs["gdn_dt_bias"]).reshape(depth, 8)
    p["gdn_norm_g"] = f(inputs["gdn_norm_g"])
    p["w_gate"] = f(inputs["w_gate"]).reshape(depth, D_MODEL, 3072)
    p["w_branch_out"] = f(inputs["w_branch_out"]).reshape(depth, 1536, D_MODEL)
    p["w_out"] = f(inputs["w_out"])
    p["final_norm_g"] = f(inputs["final_norm_g"]).reshape(1, D_MODEL)
    return p


def run(inputs, n_cores, nseq, nch, depth):
    x = np.asarray(inputs["x"], dtype=np.float32)
    B, S, D = x.shape
    assert B == n_cores * nseq and S == nch * 128
    nc, _ = build(nseq, nch, depth)
    params = prep_params(inputs, depth)
    consts = make_consts(nch)
    in_maps = []
    for i in range(n_cores):
        m = dict(params)
        m["consts"] = consts
        m["x"] = np.ascontiguousarray(x[i * nseq:(i + 1) * nseq].reshape(nseq * nch, 128, D))
        in_maps.append(m)
    res = run_bass_kernel_spmd(nc, in_maps, core_ids=list(range(n_cores)))
    outs = [np.asarray(r["out"]).reshape(nseq, S, D) for r in res.results]
    return np.concatenate(outs, axis=0).astype(np.float32)


def kernel(**inputs):
    return run(inputs, 8, 2, 16, 2)
```

```python
import contextlib
import numpy as np
import concourse.bass as bass
import concourse.mybir as mybir
from concourse.bass_utils import run_bass_kernel_spmd

F32 = mybir.dt.float32
BF16 = mybir.dt.bfloat16
ALU = mybir.AluOpType
AF = mybir.ActivationFunctionType

D_MODEL = 1024
IN_COLS = 5648
EPS = 1e-6
NEG = -1.0e30


class V:
    __slots__ = ("ap", "key", "gen", "hk")

    def __init__(self, ap, key=None, gen=None, hk=None):
        self.ap = ap
        self.key = key if key is not None else object()
        self.gen = gen
        self.hk = hk

    def h(self, i):
        return V(self.ap[:, i], self.hk[i] if self.hk is not None else self.key, self.gen)

    def __getitem__(self, idx):
        return V(self.ap[idx], self.key, self.gen)

    def re(self, s, **kw):
        return V(self.ap.rearrange(s, **kw), self.key, self.gen)

    def bc(self, shape, axes=()):
        ap = self.ap
        for a in axes:
            ap = ap.unsqueeze(a)
        return V(ap.to_broadcast(list(shape)), self.key, self.gen)

    def bitcast(self, dt):
        return V(self.ap.bitcast(dt), self.key, self.gen)

    @property
    def shape(self):
        return self.ap.shape


class KS(tuple):
    pass


def _keys(vs):
    out = []
    for v in vs:
        if isinstance(v, V):
            k = v.key
            if isinstance(k, KS):
                out.extend(k)
            else:
                out.append(k)
    return out


class _Op:
    __slots__ = ("eng", "fn", "dma", "deps", "sig", "need_sig", "pre")

    def __init__(self, eng, fn, dma):
        self.eng = eng
        self.fn = fn
        self.dma = dma
        self.deps = []
        self.sig = None
        self.need_sig = dma
        self.pre = None


ENGS = ("pe", "act", "dve", "pool", "sp")
NDMASEM = 16


class Prog:
    def __init__(self, nc):
        self.nc = nc
        self.ops = []
        self.last_w = {}
        self.readers = {}
        self.bank_gen = {}
        self.excl = set()

    def _rec(self, eng, fn, reads, writes, dma=False):
        op = _Op(eng, fn, dma)
        for v in list(reads) + list(writes):
            if isinstance(v, V) and v.gen is not None:
                assert self.bank_gen[v.gen[0]] == v.gen[1], ("stale tile use (buffer re-allocated)", v.gen[0])
        rk = _keys(reads)
        wk = _keys(writes)
        deps = set()
        for k in rk:
            w = self.last_w.get(k)
            if w is not None:
                deps.add(w)
            if k in self.excl:
                rd = self.readers.get(k)
                if rd:
                    for e2, o2 in rd.items():
                        if e2 != eng:
                            deps.add(o2)
        for k in wk:
            w = self.last_w.get(k)
            if w is not None:
                deps.add(w)
            rd = self.readers.get(k)
            if rd:
                deps.update(rd.values())
        for d in deps:
            if d.eng == "pe" and eng == "pe" and not d.dma and not dma:
                continue
            d.need_sig = True
            op.deps.append(d)
        for k in rk:
            rd = self.readers.setdefault(k, {})
            rd[id(op) if dma else eng] = op
        for k in wk:
            self.last_w[k] = op
            self.readers[k] = {}
        self.ops.append(op)
        return op

    def mm(self, out, lhsT, rhs, start=True, stop=True):
        self._rec("pe", lambda e: e.matmul(out.ap, lhsT.ap, rhs.ap, start=start, stop=stop),
                  [lhsT, rhs], [out])

    def tr(self, out, in_, ident):
        self._rec("pe", lambda e: e.transpose(out.ap, in_.ap, ident.ap), [in_, ident], [out])

    def act(self, out, in_, func, bias=None, scale=None, accum=None):
        kw = {}
        reads = [in_]
        writes = [out]
        if bias is not None:
            kw["bias"] = bias.ap if isinstance(bias, V) else bias
            reads.append(bias)
        if scale is not None:
            kw["scale"] = scale.ap if isinstance(scale, V) else scale
            reads.append(scale)
        if accum is not None:
            kw["accum_out"] = accum.ap
            writes.append(accum)
        self._rec("act", lambda e: e.activation(out.ap, in_.ap, func, **kw), reads, writes)

    def tt(self, eng, out, a, b, op):
        self._rec(eng, lambda e: e.tensor_tensor(out.ap, a.ap, b.ap, op), [a, b], [out])

    def ts(self, eng, out, a, s1, op0, s2=None, op1=None):
        s1a = s1.ap if isinstance(s1, V) else s1
        s2a = s2.ap if isinstance(s2, V) else s2
        kw = {}
        if op1 is not None:
            kw["op1"] = op1
        self._rec(eng, lambda e: e.tensor_scalar(out.ap, a.ap, s1a, s2a, op0, **kw),
                  [a, s1, s2], [out])

    def stt(self, eng, out, a, s, b, op0, op1):
        sa = s.ap if isinstance(s, V) else s
        self._rec(eng, lambda e: e.scalar_tensor_tensor(out.ap, a.ap, sa, b.ap, op0, op1),
                  [a, s, b], [out])

    def copy(self, eng, out, a):
        if eng == "act":
            self._rec("act", lambda e: e.copy(out.ap, a.ap), [a], [out])
        else:
            self._rec(eng, lambda e: e.tensor_copy(out.ap, a.ap), [a], [out])

    def memset(self, eng, out, val):
        self._rec(eng, lambda e: e.memset(out.ap, val), [], [out])

    def dma(self, q, out, in_, **kw):
        self._rec(q, lambda e: e.dma_start(out=out.ap, in_=in_.ap, **kw), [in_], [out], dma=True)

    def emit(self, stack):
        nc = self.nc
        sems = {}
        for e in ("pe", "act", "dve", "pool"):
            sems[e] = stack.enter_context(nc.semaphore("s_" + e))
        dsem = {}
        for q in ("sp", "act", "pool"):
            dsem[q] = [stack.enter_context(nc.semaphore("d_%s%d" % (q, i))) for i in range(NDMASEM)]
        cnt = {e: 0 for e in sems}
        dn = {q: 0 for q in dsem}
        dcnt = {q: [0] * NDMASEM for q in dsem}
        for op in self.ops:
            if op.dma:
                q = op.eng
                slot = dn[q] % NDMASEM
                dn[q] += 1
                prev = dcnt[q][slot]
                dcnt[q][slot] += 16
                op.sig = (dsem[q][slot], dcnt[q][slot])
                op.pre = (dsem[q][slot], prev) if prev > 0 else None
            elif op.need_sig:
                cnt[op.eng] += 1
                op.sig = (sems[op.eng], cnt[op.eng])
        per = {e: [] for e in ENGS}
        for op in self.ops:
            per[op.eng].append(op)
        finals = []
        for q in dsem:
            for i in range(NDMASEM):
                if dcnt[q][i] > 0:
                    finals.append((dsem[q][i], dcnt[q][i]))
        self.stats = {e: len(per[e]) for e in ENGS}

        def run(engname, e):
            waited = {}
            for op in per[engname]:
                need = {}
                if op.pre is not None:
                    need[id(op.pre[0])] = op.pre
                for d in op.deps:
                    s, v = d.sig
                    if need.get(id(s), (None, 0))[1] < v:
                        need[id(s)] = (s, v)
                for sid, (s, v) in need.items():
                    if waited.get(sid, 0) >= v:
                        continue
                    waited[sid] = v
                    e.wait_ge(s, v)
                ins = op.fn(e)
                if op.sig is not None:
                    ins.then_inc(op.sig[0], 16 if op.dma else 1)
            if engname == "sp":
                for (s, v) in finals:
                    e.wait_ge(s, v)

        block = stack.enter_context(nc.Block())

        @block.tensor
        def _(e):
            run("pe", e)

        @block.scalar
        def _(e):
            run("act", e)

        @block.vector
        def _(e):
            run("dve", e)

        @block.gpsimd
        def _(e):
            run("pool", e)

        @block.sync
        def _(e):
            run("sp", e)


C_ID, C_TRIF, C_TRIB, C_NEGF, C_NEGB, C_OFFD, C_PM, C_QM = [i * 128 for i in range(8)]
C_POSC = 1024
C_POSR = 1032
C_COS = C_POSR + 512
def _const_cols(nch):
    return C_COS + 3 * nch * 64


def make_consts(nch):
    j = np.arange(128)[:, None].astype(np.float64)
    i = np.arange(128)[None, :].astype(np.float64)
    c = np.zeros((128, _const_cols(nch)), np.float32)
    c[:, C_ID:C_ID + 128] = (j == i)
    c[:, C_TRIF:C_TRIF + 128] = (j <= i)
    c[:, C_TRIB:C_TRIB + 128] = (j >= i)
    c[:, C_NEGF:C_NEGF + 128] = np.where(i >= j, 0.0, NEG)
    c[:, C_NEGB:C_NEGB + 128] = np.where(i <= j, 0.0, NEG)
    c[:, C_OFFD:C_OFFD + 128] = (j != i)
    c[:, C_PM:C_PM + 128] = np.maximum(i - j, 0)
    c[:, C_QM:C_QM + 128] = np.maximum(j - i, 0)
    p = np.arange(128).astype(np.float64)
    c[:, C_POSC + 0] = p + 1
    c[:, C_POSC + 1] = 127 - p
    c[:, C_POSC + 2] = 128 - p
    c[:, C_POSC + 3] = p
    c[:, C_POSR + 0:C_POSR + 128] = (p + 1)[None, :]
    c[:, C_POSR + 128:C_POSR + 256] = (127 - p)[None, :]
    c[:, C_POSR + 256:C_POSR + 384] = (128 - p)[None, :]
    c[:, C_POSR + 384:C_POSR + 512] = p[None, :]
    inv = (10000.0 ** (-np.arange(0, 128, 2, dtype=np.float32) / np.float32(128))).astype(np.float32)
    pos = (np.arange(nch)[None, :] * 128 + np.arange(128)[:, None]).astype(np.float32)
    ang = (pos[:, :, None] * inv[None, None, :]).astype(np.float32)
    cos = np.cos(ang).astype(np.float32).reshape(128, nch * 64)
    sin = np.sin(ang).astype(np.float32).reshape(128, nch * 64)
    c[:, C_COS:C_COS + nch * 64] = cos
    c[:, C_COS + nch * 64:C_COS + 2 * nch * 64] = sin
    c[:, C_COS + 2 * nch * 64:C_COS + 3 * nch * 64] = -sin
    return c


def build(nseq, nch, depth):
    CHAIN_LSW = 5
    PREC = 0
    S = nch * 128
    NT = nseq * nch
    nc = bass.Bass("TRN2", target_bir_lowering=False)
    st = contextlib.ExitStack()

    def din(name, shape, dt=F32):
        return V(nc.dram_tensor(name, list(shape), dt, kind="ExternalInput").ap())

    def dscr(name, shape, dt):
        return V(nc.dram_tensor(name, list(shape), dt, kind="Internal").ap())

    x_d = din("x", [NT, 128, D_MODEL])
    out_d = V(nc.dram_tensor("out", [NT, 128, D_MODEL], F32, kind="ExternalOutput").ap())
    consts_d = din("consts", [128, _const_cols(nch)])
    norm_g_d = din("norm_g", [depth, D_MODEL])
    w_in_d = din("w_in", [depth, D_MODEL, IN_COLS])
    gm_ln_g_d = din("gm_ln_g", [depth, 512])
    gm_ln_b_d = din("gm_ln_b", [depth, 512])
    gm_w_s_d = din("gm_w_s", [depth, 4, 128, 128])
    gm_b_s_d = din("gm_b_s", [depth, 512])
    ret_dl_d = din("ret_decay_logit", [depth, 8])
    ret_ng_d = din("ret_norm_g", [depth, 512])
    conv_w_d = din("gdn_conv_w", [depth, 3, 1536])
    a_log_d = din("gdn_a_log", [depth, 8])
    dtb_d = din("gdn_dt_bias", [depth, 8])
    gdn_ng_d = din("gdn_norm_g", [depth, 128])
    w_gate_d = din("w_gate", [depth, D_MODEL, 3072])
    w_bo_d = din("w_branch_out", [depth, 1536, D_MODEL])
    w_out_d = din("w_out", [depth, D_MODEL, D_MODEL])
    fng_d = din("final_norm_g", [1, D_MODEL])

    xres_d = dscr("xres", [NT, 128, D_MODEL], F32)
    Y_d = [dscr("y%d" % n, [nch, 128, 512], BF16) for n in range(3)]
    ACC_d = dscr("accm", [nch, 128, 1024], F32)
    RZ_d = dscr("rz", [nch, 128, 512], BF16)
    GZ_d = dscr("gz", [nch, 128, 512], BF16)
    ROF_d = dscr("rof", [nch, 128, 512], F32)
    GOF_d = dscr("gof", [nch, 128, 512], F32)
    RQDB_d = dscr("rqdb", [nch, 128, 512], BF16)
    RKDB_d = dscr("rkdb", [nch, 128, 512], BF16)
    RV_d = dscr("rv", [nch, 128, 512], BF16)
    GKQ_d = dscr("gkq", [nch, 128, 1024], BF16)
    GVT_d = dscr("gvt", [nch, 128, 512], BF16)
    GKDB_d = dscr("gkdb", [nch, 128, 512], BF16)
    GSC_d = dscr("gsc", [nch, 128, 56], F32)

    def heads(v, A):
        ks = [object() for _ in range(A)]
        return V(v.ap, KS(ks), None, [KS((k,)) for k in ks])

    def sb(name, shape, dt=F32):
        return V(st.enter_context(nc.sbuf_tensor("sb_" + name, list(shape), dt))[:])

    P = Prog(nc)
    banks = [V(st.enter_context(nc.psum_tensor("ps%d" % i, [128, 512], F32))[:]) for i in range(8)]
    bank_i = [0]
    for b in banks:
        P.excl.add(b.key)

    ALLB = {"banks": list(range(8)), "i": 0}
    gen_ctr = [0]

    def bgroup(ids):
        return {"banks": list(ids), "i": 0}

    def ps(shape, dt=F32, g=None):
        g = g or ALLB
        bi = g["banks"][g["i"] % len(g["banks"])]
        g["i"] += 1
        gen_ctr[0] += 1
        P.bank_gen[bi] = gen_ctr[0]
        b = V(banks[bi].ap, banks[bi].key, (bi, gen_ctr[0]))
        if dt == BF16:
            b = b.bitcast(BF16)
        n = int(np.prod(shape[1:]))
        v = b[:, 0:n]
        if len(shape) == 3:
            v = v.re("p (a b) -> p a b", a=shape[1])
        elif len(shape) == 4:
            v = v.re("p (a b c) -> p a b c", a=shape[1], b=shape[2])
        return v

    rings = {}
    NU = 86
    ARENA = st.enter_context(nc.sbuf_tensor("sb_arena", [128, NU * 256], F32))[:]
    ukeys = [("U", i) for i in range(NU * 4)]
    bump = [0]

    limit = [NU]

    def new_phase(reserve=0):
        for k in [k for k in rings if not k.startswith("!")]:
            del rings[k]
        bump[0] = 0
        limit[0] = (NU - reserve) * 4

    def tmp(name, shape, dt=F32, n=1):
        nel = int(np.prod(shape[1:]))
        nbytes = nel * (4 if dt == F32 else 2)
        if nbytes < 512:
            name = "!" + name
            if name not in rings:
                rings[name] = [[sb("%s_%d" % (name[1:], i), shape, dt) for i in range(n)], 0]
        elif name not in rings:
            units = (nbytes + 255) // 256
            bufs = []
            for i in range(n):
                u0 = bump[0]
                bump[0] += units
                assert bump[0] <= limit[0], ("arena overflow", name, bump[0], limit[0])
                v = V(ARENA[:, u0 * 64:(u0 + units) * 64], KS(ukeys[u0:u0 + units]))
                if dt != F32:
                    v = v.bitcast(BF16)
                v = v[:, 0:nel]
                if len(shape) == 3:
                    v = v.re("p (a b) -> p a b", a=shape[1])
                elif len(shape) == 4:
                    v = v.re("p (a b c) -> p a b c", a=shape[1], b=shape[2])
                if len(shape) >= 3 and (nbytes // shape[1]) % 256 == 0 and nbytes % shape[1] == 0:
                    k = (nbytes // shape[1]) // 256
                    v.hk = [KS(ukeys[u0 + j * k:u0 + (j + 1) * k]) for j in range(shape[1])]
                bufs.append(v)
            rings[name] = [bufs, 0]
        r = rings[name]
        i = r[1] % len(r[0])
        v = r[0][i]
        r[1] += 1
        gen_ctr[0] += 1
        gid = ("ring", name, i)
        P.bank_gen[gid] = gen_ctr[0]
        return V(v.ap, v.key, (gid, gen_ctr[0]), v.hk)

    def run_window(genfn, items, K, lag):
        groups = [bgroup(range(i * (8 // K), (i + 1) * (8 // K))) for i in range(K)]
        items = list(items)
        active = []
        nxt = 0
        while active or nxt < len(items):
            if nxt < len(items) and len(active) < K and (not active or active[-1][1] >= lag):
                active.append([genfn(items[nxt], groups[nxt % K]), 0])
                nxt += 1
            for a in list(active):
                try:
                    next(a[0])
                    a[1] += 1
                except StopIteration:
                    active.remove(a)

    def run_gens(gens):
        gens = list(gens)
        while gens:
            for g in list(gens):
                try:
                    next(g)
                except StopIteration:
                    gens.remove(g)

    CT = sb("consts", [128, _const_cols(nch)])
    P.dma("sp", CT, consts_d)
    IDF = CT[:, C_ID:C_ID + 128]
    TRIF = CT[:, C_TRIF:C_TRIF + 128]
    TRIB = CT[:, C_TRIB:C_TRIB + 128]
    NEGM = [CT[:, C_NEGF:C_NEGF + 128], CT[:, C_NEGB:C_NEGB + 128]]
    OFFD = CT[:, C_OFFD:C_OFFD + 128]
    PM = CT[:, C_PM:C_PM + 128]
    QM = CT[:, C_QM:C_QM + 128]
    POSC = CT[:, C_POSC:C_POSC + 4]
    POSR = CT[:, C_POSR:C_POSR + 512].re("p (a b) -> p a b", a=4)
    COS = CT[:, C_COS:C_COS + nch * 64].re("p (c f) -> p c f", c=nch)
    SIN = CT[:, C_COS + nch * 64:C_COS + 2 * nch * 64].re("p (c f) -> p c f", c=nch)
    NSIN = CT[:, C_COS + 2 * nch * 64:C_COS + 3 * nch * 64].re("p (c f) -> p c f", c=nch)
    IDB = sb("idb", [128, 128], BF16)
    P.copy("dve", IDB, IDF)
    ONESF = sb("onesf", [128, 128])
    P.memset("pool", ONESF, 1.0)
    ONESB = sb("onesb", [128, 128], BF16)
    P.memset("pool", ONESB, 1.0)
    ONESDIV = sb("onesdiv", [128, 128])
    P.memset("pool", ONESDIV, 1.0 / 128)

    hT = sb("hT", [128, 8, S + 2], BF16)
    P.memset("pool", hT[:, :, 0:1], 0.0)
    P.memset("pool", hT[:, :, S + 1:S + 2], 0.0)
    WBUF = sb("wbuf", [128, 8 * 2560], BF16)

    NG = sb("ng", [128, 1024])
    FNG = sb("fng", [128, 1024])
    LNG = sb("lng", [128, 512])
    LNB = sb("lnb", [128, 512])
    BS = sb("bs", [128, 512])
    WST = sb("wst", [128, 4, 128], BF16)
    PAR8 = sb("par8", [128, 3, 8])
    LG = sb("lg", [128, 8])
    TMP8 = [sb("tmp8_%d" % i, [128, 8]) for i in range(3)]
    MASKT = sb("maskt", [128, 4, 128])
    RBQF = sb("rbqf", [128, 4, 128], BF16)
    RBQB = sb("rbqb", [128, 4, 128], BF16)
    KD = sb("kd", [128, 8])
    CH = sb("ch", [128, 8])
    RNG = sb("rng", [128, 4])
    GNG = sb("gng", [128, 1])
    CW = sb("cw", [128, 12, 3])
    NAR = sb("nar", [128, 8])
    DTB = PAR8[:, 2, :]

    P.dma("act", FNG, V(fng_d.ap[0].partition_broadcast(128), fng_d.key))

    def load_layer_params(l):
        new_phase()
        WS32 = tmp("ws32", [128, 4, 128])
        CW3 = tmp("cw3", [128, 1536])[0:3, :]
        def bcast(dst, src_row):
            P.dma("act", dst, V(src_row.ap.partition_broadcast(128), src_row.key))
        bcast(NG, norm_g_d[l])
        bcast(LNG, gm_ln_g_d[l])
        bcast(LNB, gm_ln_b_d[l])
        bcast(BS, gm_b_s_d[l])
        bcast(PAR8[:, 0, :], ret_dl_d[l])
        bcast(PAR8[:, 1, :], a_log_d[l])
        bcast(PAR8[:, 2, :], dtb_d[l])
        P.dma("act", WS32, gm_w_s_d[l].re("g p q -> p g q"))
        P.dma("act", RNG, ret_ng_d[l].re("(h d) -> d h", h=4), allow_slow_non_contiguous=True)
        P.dma("act", GNG, gdn_ng_d[l].re("(d o) -> d o", o=1), allow_slow_non_contiguous=True)
        P.dma("act", CW3, conv_w_d[l])
        pt = ps([128, 4, 128])
        for g in range(4):
            P.tr(pt[:, g], WS32[:, g], IDF)
        P.copy("dve", WST, pt)
        pc = ps([128, 12, 3])
        for t in range(12):
            P.tr(pc[:, t], CW3[:, t * 128:(t + 1) * 128], IDF[0:3, 0:3])
        P.copy("dve", CW, pc)
        xdl = PAR8[:, 0, :]
        P.act(TMP8[0], xdl, AF.Abs)
        P.act(TMP8[0], TMP8[0], AF.Exp, scale=-1.0)
        P.act(TMP8[0], TMP8[0], AF.Ln, bias=1.0)
        P.ts("dve", TMP8[1], xdl, 0.0, ALU.min)
        P.tt("dve", LG, TMP8[1], TMP8[0], ALU.subtract)
        for h in range(4):
            e1 = tmp("mk_e1", [128, 128])
            P.ts("dve", e1, PM, LG[:, h:h + 1], ALU.mult)
            e2 = tmp("mk_e2", [128, 128])
            P.stt("dve", e2, QM, LG[:, 4 + h:5 + h], e1, ALU.mult, ALU.add)
            P.act(MASKT[:, h], e2, AF.Exp)
            P.act(RBQF[:, h], POSR[:, 0], AF.Exp, scale=LG[:, h:h + 1])
            P.act(RBQB[:, h], POSR[:, 2], AF.Exp, scale=LG[:, 4 + h:5 + h])
            P.act(KD[:, h:h + 1], POSC[:, 1:2], AF.Exp, scale=LG[:, h:h + 1])
            P.act(KD[:, 4 + h:5 + h], POSC[:, 3:4], AF.Exp, scale=LG[:, 4 + h:5 + h])
        P.act(CH, LG, AF.Exp, scale=128.0)
        P.act(NAR, PAR8[:, 1, :], AF.Exp)
        P.ts("dve", NAR, NAR, -1.0, ALU.mult)

    top_live = [0]

    def wregion(kind, nunits):
        if kind == "WBUF":
            return WBUF
        u0 = NU - nunits
        v = V(ARENA[:, u0 * 256:NU * 256], KS(ukeys[u0 * 4:NU * 4]))
        return v.bitcast(BF16)

    def load_w(src2d, ncols, c0, region, kcs=8, off=0):
        view = region[:, off:off + kcs * ncols].re("p (k n) -> p k n", k=kcs)
        src = src2d.re("(k p) n -> p k n", p=128)
        step = 2 if kcs % 2 == 0 else kcs
        for k0 in range(0, kcs, step):
            P.dma("pool", view[:, k0:k0 + step, :], src[:, k0:k0 + step, c0:c0 + ncols])
        return view

    WPLAN = {"gmlp": ("WBUF", 0), "ret": ("TOP", 32), "gdn": ("WBUF", 0),
             "m0": ("TOP", 24), "m1": ("WBUF", 0), "m2": ("TOP", 40)}

    conv_jobs = []
    WB16 = {}

    def add_conv(key, src2d):
        rows, cols = src2d.shape
        dst = dscr("wb_%s_%d" % key, [rows, cols], BF16)
        WB16[key] = dst
        conv_jobs.append((key, src2d, dst))

    for l_ in range(depth):
        add_conv(("gmlp", l_), w_in_d[l_][:, 0:1536])
        add_conv(("ret", l_), w_in_d[l_][:, 1536:3584])
        add_conv(("gdn", l_), w_in_d[l_][:, 3584:5648])
        for n_ in range(3):
            add_conv(("g%d" % n_, l_), w_gate_d[l_][:, n_ * 1024:(n_ + 1) * 1024])
            add_conv(("b%d" % n_, l_), w_bo_d[l_][n_ * 512:(n_ + 1) * 512, :])
            if n_ == 2:
                add_conv(("out", l_), w_out_d[l_])
    conv_pos = [0]

    def conv_upto(idx):
        while conv_pos[0] <= idx and conv_pos[0] < len(conv_jobs):
            key, src2d, dst = conv_jobs[conv_pos[0]]
            conv_pos[0] += 1
            sv = src2d.re("(k p) n -> p k n", p=128)
            dv = dst.re("(k p) n -> p k n", p=128)
            kcs = sv.shape[1]
            for k0 in range(0, kcs, 2):
                P.dma("pool", dv[:, k0:k0 + 2, :], sv[:, k0:k0 + 2, :])

    def load_wb(key, region, off=0):
        idx = [i for i, j in enumerate(conv_jobs) if j[0] == key][0]
        conv_upto(idx + 2)
        src = WB16[key]
        rows, cols = src.shape
        kcs = rows // 128
        view = region[:, off:off + kcs * cols].re("p (k n) -> p k n", k=kcs)
        sv = src.re("(k p) n -> p k n", p=128)
        for k0 in range(0, kcs, 2):
            P.dma("sp", view[:, k0:k0 + 2, :], sv[:, k0:k0 + 2, :])
        return view

    def issue_w(tag, l):
        kind, nu = WPLAN[tag]
        reg = wregion(kind, nu)
        if tag in ("gmlp", "ret", "gdn"):
            return load_wb((tag, l), reg)
        n = int(tag[1])
        WG = load_wb(("g%d" % n, l), reg)
        WB = load_wb(("b%d" % n, l), reg, off=8192)
        WO = load_wb(("out", l), reg, off=12288) if n == 2 else None
        return (WG, WB, WO)

    def rstd_from(dst, src, scale, eps):
        P.act(dst, src, AF.Ln, bias=eps, scale=scale)
        P.act(dst, dst, AF.Exp, scale=-0.5)

    def phase0(l, s):
        new_phase()
        src = x_d if l == 0 else xres_d
        for c in range(nch):
            xt = tmp("x32", [128, 1024], F32, 2)
            P.dma("sp", xt, src[s * nch + c])
            junk = tmp("junkb", [128, 1024], BF16, 2)
            ss = tmp("ss", [128, 1], F32, 2)
            P.act(junk, xt, AF.Square, accum=ss)
            rs = tmp("rs", [128, 1], F32, 2)
            rstd_from(rs, ss, 1.0 / D_MODEL, EPS)
            hb = tmp("hb", [128, 1024], BF16, 2)
            P.stt("dve", hb, xt, rs, NG, ALU.mult, ALU.mult)
            for half in range(2):
                pt = ps([128, 4, 128], BF16)
                for k in range(4):
                    kk = half * 4 + k
                    P.tr(pt[:, k], hb[:, kk * 128:(kk + 1) * 128], IDB)
                P.copy("act" if half == 0 else "dve",
                       hT[:, half * 4:(half + 1) * 4, 1 + c * 128:1 + (c + 1) * 128], pt)

    def tok(c):
        return slice(1 + c * 128, 1 + (c + 1) * 128)

    def proj_fm(W, col0, c, nt=4, g=None):
        p = ps([128, nt, 128], g=g)
        for t in range(nt):
            for kc in range(8):
                P.mm(p[:, t], W[:, kc, col0 + t * 128:col0 + (t + 1) * 128], hT[:, kc, tok(c)],
                     start=(kc == 0), stop=(kc == 7))
        return p

    def proj_tm(W, col0, c, n=512, g=None):
        p = ps([128, n], g=g)
        for kc in range(8):
            P.mm(p, hT[:, kc, tok(c)], W[:, kc, col0:col0 + n], start=(kc == 0), stop=(kc == 7))
        return p

    def phase_gmlp(l, s, W, pre):
        new_phase(32)
        pre()

        def gen(c, g):
            pu = proj_fm(W, 0, c, g=g)
            pv = proj_tm(W, 512, c, g=g)
            yield
            gu = tmp("gu", [128, 4, 128], BF16, 2)
            P.act(gu, pu, AF.Gelu)
            gv = tmp("gv", [128, 512], F32, 2)
            vsum = tmp("vsum", [128, 1], F32, 2)
            P.act(gv, pv, AF.Gelu, accum=vsum)
            pz = proj_fm(W, 1024, c, g=g)
            yield
            sz = tmp("sz", [128, 4, 128], BF16, 2)
            P.act(sz, pz, AF.Silu)
            yield
            ug = tmp("ug", [128, 4, 128], BF16, 2)
            P.tt("pool", ug, gu, sz, ALU.mult)
            nmean = tmp("nmean", [128, 1], F32, 2)
            P.ts("dve", nmean, vsum, -1.0 / 512, ALU.mult)
            yield
            cen = tmp("cen", [128, 512], F32, 2)
            P.ts("dve", cen, gv, nmean, ALU.add)
            yield
            junk = tmp("junkb", [128, 1024], BF16, 2)
            vss = tmp("vss", [128, 1], F32, 2)
            P.act(junk[:, 0:512], cen, AF.Square, accum=vss)
            yield
            vr = tmp("vr", [128, 1], F32, 2)
            P.act(vr, vss, AF.Ln, bias=EPS, scale=1.0 / 512)
            yield
            P.act(vr, vr, AF.Exp, scale=-0.5)
            yield
            t1 = tmp("gm_t1", [128, 512], F32, 2)
            P.stt("dve", t1, cen, vr, LNG, ALU.mult, ALU.mult)
            yield
            vn = tmp("vn", [128, 512], BF16, 2)
            P.tt("pool", vn, t1, LNB, ALU.add)
            yield
            pm = ps([128, 4, 128], g=g)
            for gg in range(4):
                P.mm(pm[:, gg], vn[:, gg * 128:(gg + 1) * 128], WST[:, gg])
            yield
            t2 = tmp("gm_t2", [128, 4, 128], F32, 2)
            P.tt("dve", t2, pm, BS.re("p (g q) -> p g q", g=4), ALU.add)
            yield
            ya = tmp("ya", [128, 4, 128], BF16, 2)
            P.tt("pool", ya, t2, ug, ALU.mult)
            P.dma("sp", Y_d[0][c], ya.re("p a b -> p (a b)"))
            yield

        run_window(gen, range(nch), 2, 6)

    def rotary(src32, c, name):
        q4 = src32.re("p (h t f) -> p h t f", h=4, t=2)
        cosb = COS[:, c, :].bc([128, 4, 2, 64], axes=(1, 1))
        t1 = tmp("rot_t1", [128, 4, 2, 64])
        P.tt("pool", t1, q4, cosb, ALU.mult)
        t2 = tmp("rot_t2", [128, 4, 2, 64])
        P.tt("dve", t2[:, :, 0, :], q4[:, :, 1, :], NSIN[:, c, :].bc([128, 4, 64], axes=(1,)), ALU.mult)
        P.tt("dve", t2[:, :, 1, :], q4[:, :, 0, :], SIN[:, c, :].bc([128, 4, 64], axes=(1,)), ALU.mult)
        qr = tmp(name, [128, 512], BF16)
        P.tt("pool", qr.re("p (h t f) -> p h t f", h=4, t=2), t1, t2, ALU.add)
        return qr

    def transpose4(src, name, eng="act", g=None):
        pt = ps([128, 4, 128], BF16, g=g)
        for h in range(4):
            blk = src[:, h] if len(src.shape) == 3 else src[:, h * 128:(h + 1) * 128]
            P.tr(pt[:, h], blk, IDB)
        if name is None:
            return pt
        o = tmp(name, [128, 4, 128], BF16)
        P.copy(eng, o, pt)
        return o

    def state_update(Sx, Sb, lhs, rhs, dec_bc, g=None):
        pS = ps([128, 4, 128], g=g)
        for h in range(4):
            P.mm(pS[:, h], lhs[:, h], rhs[:, h])
        P.tt("pool", Sx, Sx, dec_bc, ALU.mult)
        P.tt("dve", Sx, Sx, pS, ALU.add)
        P.copy("act", Sb, Sx)

    SF = sb("SF", [128, 4, 128])
    SFB = sb("SFB", [128, 4, 128], BF16)
    SFB2 = [SFB, sb("SFB2", [128, 4, 128], BF16)]
    SFL2 = [sb("SFL", [128, 4, 128], BF16), sb("SFL2", [128, 4, 128], BF16)] if PREC & 2 else [None, None]

    def rotary_gen(src32, c, name):
        q4 = src32.re("p (h t f) -> p h t f", h=4, t=2)
        cosb = COS[:, c, :].bc([128, 4, 2, 64], axes=(1, 1))
        t1 = tmp("rot_t1" + name, [128, 4, 2, 64], F32, 2)
        P.tt("pool" if name == "q" else "dve", t1, q4, cosb, ALU.mult)
        t2 = tmp("rot_t2" + name, [128, 4, 2, 64], F32, 2)
        P.tt("dve", t2[:, :, 0, :], q4[:, :, 1, :], NSIN[:, c, :].bc([128, 4, 64], axes=(1,)), ALU.mult)
        P.tt("dve", t2[:, :, 1, :], q4[:, :, 0, :], SIN[:, c, :].bc([128, 4, 64], axes=(1,)), ALU.mult)
        return t1, t2

    def phase_ret_fwd(l, s, W, pre):
        new_phase(32)
        pre()
        P.memset("pool", SF, 0.0)
        P.memset("pool", SFB2[0], 0.0)
        if PREC & 2:
            P.memset("pool", SFL2[0], 0.0)

        def gen(c, g):
            S_old, S_new = SFB2[c % 2], SFB2[(c + 1) % 2]
            L_old, L_new = SFL2[c % 2], SFL2[(c + 1) % 2]
            pq = proj_tm(W, 0, c, g=g)
            pk = proj_tm(W, 512, c, g=g)
            yield
            q32 = tmp("q32", [128, 512], F32, 2)
            P.copy("act", q32, pq)
            k32 = tmp("k32", [128, 512], F32, 2)
            P.act(k32, pk, AF.Copy, scale=float(128 ** -0.5))
            pv = proj_tm(W, 1024, c, g=g)
            pz = proj_fm(W, 1536, c, g=g)
            yield
            vb = tmp("r_vb", [128, 4, 128], BF16, 2)
            P.copy("dve", vb.re("p a b -> p (a b)"), pv)
            szr = tmp("r_sz", [128, 4, 128], BF16, 2)
            P.act(szr, pz, AF.Silu)
            P.dma("sp", RZ_d[c], szr.re("p a b -> p (a b)"))
            P.dma("sp", RV_d[c], vb.re("p a b -> p (a b)"))
            qa, qb = rotary_gen(q32, c, "q")
            yield
            ka, kb = rotary_gen(k32, c, "k")
            yield
            qr = tmp("r_qr", [128, 512], BF16, 2)
            P.tt("pool", qr.re("p (h t f) -> p h t f", h=4, t=2), qa, qb, ALU.add)
            kr = tmp("r_kr", [128, 512], BF16, 2)
            P.tt("dve", kr.re("p (h t f) -> p h t f", h=4, t=2), ka, kb, ALU.add)
            yield
            ptq = transpose4(qr, None, g=g)
            ptk = transpose4(kr, None, g=g)
            kr4 = kr.re("p (h d) -> p h d", h=4)
            kdf = tmp("r_kdf", [128, 4, 128], BF16, 2)
            P.tt("dve", kdf, kr4, KD[:, 0:4].bc([128, 4, 128], axes=(2,)), ALU.mult)
            kdb = tmp("r_kdb", [128, 4, 128], BF16, 2)
            P.tt("pool", kdb, kr4, KD[:, 4:8].bc([128, 4, 128], axes=(2,)), ALU.mult)
            P.dma("sp", RKDB_d[c], kdb.re("p a b -> p (a b)"))
            yield
            qT = tmp("r_qT", [128, 4, 128], BF16, 2)
            P.copy("act", qT, ptq)
            kT = tmp("r_kT", [128, 4, 128], BF16, 2)
            P.copy("dve", kT, ptk)
            yield
            psc = ps([128, 4, 128], g=g)
            for h in range(4):
                P.mm(psc[:, h], kT[:, h], qT[:, h])
            qdf = tmp("r_qdf", [128, 4, 128], BF16, 2)
            P.tt("pool", qdf, qT, RBQF, ALU.mult)
            qdb = tmp("r_qdb", [128, 4, 128], BF16, 2)
            P.tt("pool", qdb, qT, RBQB, ALU.mult)
            P.dma("sp", RQDB_d[c], qdb.re("p a b -> p (a b)"))
            yield
            pT = tmp("r_pT", [128, 4, 128], BF16, 2)
            P.tt("dve", pT, psc, MASKT, ALU.mult)
            yield
            po = ps([128, 4, 128], g=g)
            for h in range(4):
                P.mm(po[:, h], vb[:, h], pT[:, h], start=True, stop=False)
                if PREC & 2:
                    P.mm(po[:, h], L_old[:, h], qdf[:, h], start=False, stop=False)
                P.mm(po[:, h], S_old[:, h], qdf[:, h], start=False, stop=True)
            pS = ps([128, 4, 128], g=g)
            for h in range(4):
                P.mm(pS[:, h], kdf[:, h], vb[:, h])
            P.tt("pool", SF, SF, CH[:, 0:4].bc([128, 4, 128], axes=(2,)), ALU.mult)
            yield
            P.tt("dve", SF, SF, pS, ALU.add)
            of32 = tmp("r_of", [128, 4, 128], F32, 2)
            P.copy("act", of32, po)
            P.dma("sp", ROF_d[c], of32.re("p a b -> p (a b)"))
            yield
            P.copy("act", S_new, SF)
            yield
            if PREC & 2:
                P.tt("pool", L_new, SF, S_new, ALU.subtract)
                yield

        run_window(gen, range(nch), 2, 6)

    def load_t(q, name, shape, dt, src, n=2):
        t = tmp(name, shape, dt, n)
        if len(shape) == 3:
            P.dma(q, t.re("p a b -> p (a b)"), src)
        elif len(shape) == 4:
            P.dma(q, t.re("p a b c -> p (a b c)"), src)
        else:
            P.dma(q, t, src)
        return t

    def ret_bwd_gen(l, s, g):
        P.memset("pool", SF, 0.0)
        P.memset("pool", SFB, 0.0)
        if PREC & 2:
            P.memset("pool", SFL2[0], 0.0)
        def rloads(c):
            return (load_t("sp", "b_qdb", [128, 4, 128], BF16, RQDB_d[c]),
                    load_t("sp", "b_kdb", [128, 4, 128], BF16, RKDB_d[c]),
                    load_t("sp", "b_v", [128, 4, 128], BF16, RV_d[c]),
                    load_t("sp", "b_of", [128, 4, 128], F32, ROF_d[c]),
                    load_t("sp", "b_sz", [128, 4, 128], BF16, RZ_d[c]))

        nxt_l = rloads(nch - 1)
        for c in reversed(range(nch)):
            qdb, kdb, vb, of32, szr = nxt_l
            if c > 0:
                nxt_l = rloads(c - 1)
            pob = ps([128, 4, 128], g=g)
            for h in range(4):
                if PREC & 2:
                    P.mm(pob[:, h], SFB[:, h], qdb[:, h], start=True, stop=False)
                    P.mm(pob[:, h], SFL2[0][:, h], qdb[:, h], start=False, stop=True)
                else:
                    P.mm(pob[:, h], SFB[:, h], qdb[:, h])
            pS = ps([128, 4, 128], g=g)
            for h in range(4):
                P.mm(pS[:, h], kdb[:, h], vb[:, h])
            yield
            o = tmp("b_o", [128, 512])
            P.tt("dve", o, pob.re("p a b -> p (a b)"), of32.re("p a b -> p (a b)"), ALU.add)
            P.tt("pool", SF, SF, CH[:, 4:8].bc([128, 4, 128], axes=(2,)), ALU.mult)
            yield
            P.tt("dve", SF, SF, pS, ALU.add)
            osq = tmp("b_osq", [128, 512])
            P.act(osq, o, AF.Square)
            yield
            P.copy("act", SFB, SF)
            if PREC & 2:
                P.tt("pool", SFL2[0], SF, SFB, ALU.subtract)
            pmean = ps([128, 512], g=g)
            P.mm(pmean, ONESDIV, o)
            pmsq = ps([128, 512], g=g)
            P.mm(pmsq, ONESDIV, osq)
            yield
            cen = tmp("b_cen", [128, 512])
            P.stt("dve", cen, pmean, -1.0, o, ALU.mult, ALU.add)
            yield
            m2 = tmp("b_m2", [128, 512])
            P.act(m2, pmean, AF.Square)
            yield
            var = tmp("b_var", [128, 512])
            P.tt("dve", var, pmsq, m2, ALU.subtract)
            yield
            rstd_from(var, var, 1.0, EPS)
            yield
            t = tmp("b_t", [128, 4, 128])
            P.tt("pool", t.re("p a b -> p (a b)"), cen, var, ALU.mult)
            yield
            yb = tmp("b_yb", [128, 4, 128], BF16, 2)
            for h in range(4):
                P.stt("dve", yb.h(h), t.h(h), RNG[:, h:h + 1], szr.h(h), ALU.mult, ALU.mult)
            P.dma("sp", Y_d[1][c], yb.re("p a b -> p (a b)"))
            yield

    GS = heads(sb("GS", [128, 4, 128]), 4)
    GSB = sb("GSB", [128, 4, 128], BF16)
    GSL = sb("GSL", [128, 4, 128], BF16) if PREC & 1 else None
    GPB_d = dscr("gpb", [nch, 128, 512], BF16)
    GSTB_d = dscr("gstb", [nch, 128, 512], BF16)
    GQGB_d = dscr("gqgb", [nch, 128, 512], BF16)

    def gdn_recur(d, KQ, vtok, kd, SC, Pm, ST, qg, res, g):
        o = 4 * d
        one = len(g["banks"]) == 1
        pks = ps([128, 4, 128], g=g)
        for h in range(4):
            if PREC & 1:
                P.mm(pks[:, h], KQ[:, h, 0, :], GSB[:, h], start=True, stop=False)
                P.mm(pks[:, h], KQ[:, h, 0, :], GSL[:, h], start=False, stop=True)
            else:
                P.mm(pks[:, h], KQ[:, h, 0, :], GSB[:, h])
        yield
        r = tmp("g_r%d" % d, [128, 4, 128], BF16)
        for h in range(4):
            P.stt("dve", r.h(h), pks[:, h], SC[:, 16 + o + h:17 + o + h], vtok.h(h), ALU.mult, ALU.add)
        yield
        pvn = ps([128, 4, 128], g=g)
        for h in range(4):
            P.mm(pvn[:, h], Pm.h(h), r.h(h))
        yield
        vnew = tmp("g_vnew%d" % d, [128, 4, 128], BF16)
        for h in range(4):
            P.act(vnew.h(h), pvn[:, h], AF.Copy, scale=SC[:, 40 + o + h:41 + o + h])
        yield
        of32 = tmp("g_of%d" % d, [128, 4, 128], F32, 2)
        if one:
            po = ps([128, 4, 128], g=g)
            for h in range(4):
                P.mm(po[:, h], vnew.h(h), ST.h(h), start=True, stop=False)
                if PREC & 1:
                    P.mm(po[:, h], GSL[:, h], qg[:, h], start=False, stop=False)
                P.mm(po[:, h], GSB[:, h], qg[:, h], start=False, stop=True)
            yield
            P.copy("act", of32, po)
            pS = ps([128, 4, 128], g=g)
            for h in range(4):
                P.mm(pS[:, h], kd.h(h), vnew.h(h))
            yield
        else:
            pS = ps([128, 4, 128], g=g)
            for h in range(4):
                P.mm(pS[:, h], kd.h(h), vnew.h(h))
            po = ps([128, 4, 128], g=g)
            for h in range(4):
                P.mm(po[:, h], vnew.h(h), ST.h(h), start=True, stop=False)
                if PREC & 1:
                    P.mm(po[:, h], GSL[:, h], qg[:, h], start=False, stop=False)
                P.mm(po[:, h], GSB[:, h], qg[:, h], start=False, stop=True)
            yield
        for h in range(4):
            P.stt("dve", GS.h(h), GS.h(h), SC[:, 48 + o + h:49 + o + h], pS[:, h], ALU.mult, ALU.add)
        if not one:
            P.copy("act", of32, po)
        yield
        P.copy("act", GSB, GS)
        res["o"] = of32
        yield
        if PREC & 1:
            P.tt("pool", GSL, GS, GSB, ALU.subtract)
            yield

    F32R = mybir.dt.float32r

    def gdn_chain(d, c, I, g, R):
        o = 4 * d
        sx = str(d)
        KQ, SC = I["KQ"], I["SC"]

        def r_(v):
            return v

        dg = tmp("g_dg" + sx, [128, 4, 128])
        P.tt("pool", dg, IDF.bc([128, 4, 128], axes=(1,)), SC[:, o:o + 4].bc([128, 4, 128], axes=(2,)), ALU.mult)
        yield
        prb = ps([128, 4, 128], g=g)
        P.mm(prb.re("p a b -> p (a b)"), ONESF, dg.re("p a b -> p (a b)"))
        yield
        E = tmp("g_E" + sx, [128, 4, 128])
        for h in range(4):
            P.stt("dve", E.h(h), prb[:, h], SC[:, o + h:o + h + 1], NEGM[d], ALU.subtract, ALU.add)
        yield
        P.act(E, E, AF.Exp)
        D = E
        P.act(dg, prb, AF.Exp)
        EG = dg
        pg = [ps([128, 2, 256], g=g), ps([128, 2, 256], g=g)]
        for h in range(4):
            P.mm(pg[h // 2][:, h % 2], KQ[:, h, 0, :], KQ[:, h].re("p a b -> p (a b)"))
        yield
        Ds = tmp("g_Ds" + sx, [128, 4, 128])
        P.tt("pool", Ds, D, OFFD.bc([128, 4, 128], axes=(1,)), ALU.mult)
        nq = 2 if d == 0 else 1
        qg = tmp("g_qg" + sx, [128, 4, 128], BF16, nq)
        P.tt("pool", qg, KQ[:, :, 1, :], EG, ALU.mult)
        ST = tmp("g_ST" + sx, [128, 4, 128], BF16, nq)
        for hh in range(2):
            P.tt("dve", ST[:, 2 * hh:2 * hh + 2], pg[hh][:, :, 128:256], D[:, 2 * hh:2 * hh + 2], ALU.mult)
        yield
        Nc = tmp("g_N0" + sx, [128, 4, 128], F32)
        for h in range(4):
            P.stt("dve", r_(Nc.h(h)), pg[h // 2][:, h % 2, 0:128], SC[:, 32 + o + h:33 + o + h], Ds.h(h),
                  ALU.mult, ALU.mult)
        yield
        ptm = ps([128, 4, 128], F32, g=g)
        for h in range(4):
            P.tr(ptm[:, h], Nc.h(h), IDF)
        Pm = tmp("g_P" + sx, [128, 4, 128], F32, 2)
        P.tt("pool", r_(Pm), Nc, IDF.bc([128, 4, 128], axes=(1,)), ALU.add)
        yield
        Mc = tmp("g_M0" + sx, [128, 4, 128], F32)
        P.copy("act", r_(Mc), ptm)
        yield
        n2 = tmp("g_N2" + sx, [128, 4, 128], F32)
        mring = [dg, E]
        nring = [Ds, n2]

        LSW = CHAIN_LSW

        def asdt(v, lv):
            if lv < LSW - 1:
                return v
            return v.re("p a b -> p (a b)").bitcast(BF16)[:, 0:512].re("p (a b) -> p a b", a=4)

        pp = None
        for lv in range(1, 7):
            pm_ = ps([128, 4, 128], g=g)
            for h in range(4):
                P.mm(pm_[:, h], Nc[:, h], Mc[:, h])
            if lv < 6:
                pn_ = ps([128, 4, 128], g=g)
                for h in range(4):
                    P.mm(pn_[:, h], Mc[:, h], Nc[:, h])
            yield
            Mn = asdt(mring[lv % 2], lv)
            P.copy("act", Mn, pm_)
            if lv < 6:
                Nn = asdt(nring[lv % 2], lv)
                P.copy("act" if lv % 2 == 1 else "dve", Nn, pn_)
                Nc = Nn
            Mc = Mn
            if pp is not None:
                Pn = asdt(tmp("g_P" + sx, [128, 4, 128], F32, 2), lv)
                P.tt("dve", Pn, pp, Pm, ALU.add)
                Pm = Pn
            elif lv >= LSW - 1:
                Pn = asdt(tmp("g_P" + sx, [128, 4, 128], F32, 2), lv)
                P.copy("dve", Pn, Pm)
                Pm = Pn
            yield
            pp = ps([128, 4, 128], g=g)
            for h in range(4):
                P.mm(pp[:, h], Mc[:, h], Pm[:, h])
        yield
        if d == 0:
            Pf = tmp("g_Pfin", [128, 4, 128], BF16, 2)
            P.tt("dve", Pf, pp, Pm, ALU.add)
            R.update(Pm=Pf, ST=ST, qg=qg, KQ=KQ, SC=SC, vtok=I["vtok"], kdf=I["kdf"], c=c)
        else:
            Pf = tmp("g_Pfin1", [128, 4, 128], BF16)
            P.tt("dve", Pf, pp, Pm, ALU.add)
            P.dma("sp", GPB_d[c], Pf.re("p a b -> p (a b)"))
            P.dma("sp", GSTB_d[c], ST.re("p a b -> p (a b)"))
            P.dma("sp", GQGB_d[c], qg.re("p a b -> p (a b)"))
        yield

    def gdn_fwd_recur(R, g):
        res = {}
        for _ in gdn_recur(0, R["KQ"], R["vtok"], R["kdf"], R["SC"], R["Pm"], R["ST"], R["qg"], res, g):
            yield
        P.dma("sp", GOF_d[R["c"]], res["o"].re("p a b -> p (a b)"))
        yield

    def gdn_common(c, W, I, g):
        X = tmp("g_X", [128, 12, 128])
        cs = {}

        def conv_mm(b):
            pc = ps([128, 3, 132], g=g)
            for j in range(3):
                t = 3 * b + j
                for kc in range(8):
                    P.mm(pc[:, j, 0:130], W[:, kc, t * 128:(t + 1) * 128],
                         hT[:, kc, c * 128:c * 128 + 130], start=(kc == 0), stop=(kc == 7))
            cs[b] = [pc]

        def conv_dve(b):
            pc = cs[b][0]
            c1 = tmp("g_c1", [128, 3, 128], F32, 2)
            c2 = tmp("g_c2", [128, 3, 128])
            c3 = tmp("g_c3", [128, 3, 128])
            cwb = CW[:, 3 * b:3 * b + 3, :]
            P.tt("dve", c1, pc[:, :, 0:128], cwb[:, :, 0:1].bc([128, 3, 128]), ALU.mult)
            P.tt("dve", c2, pc[:, :, 1:129], cwb[:, :, 1:2].bc([128, 3, 128]), ALU.mult)
            P.tt("dve", c3, pc[:, :, 2:130], cwb[:, :, 2:3].bc([128, 3, 128]), ALU.mult)
            cs[b] = [c1, c2, c3]

        def conv_pool(b):
            c1, c2, c3 = cs[b]
            P.tt("pool", c1, c1, c2, ALU.add)
            P.tt("pool", c1, c1, c3, ALU.add)

        def conv_act(b):
            P.act(X[:, 3 * b:3 * b + 3, :], cs[b][0], AF.Silu)

        conv_mm(0)
        yield
        conv_dve(0)
        yield
        for b in range(1, 4):
            conv_mm(b)
            conv_pool(b - 1)
            yield
            conv_dve(b)
            conv_act(b - 1)
            yield
        conv_pool(3)
        pab = ps([128, 16], g=g)
        for kc in range(8):
            P.mm(pab, hT[:, kc, tok(c)], W[:, kc, 2048:2064], start=(kc == 0), stop=(kc == 7))
        yield
        conv_act(3)
        ab = tmp("g_ab", [128, 16])
        P.copy("dve", ab, pab)
        pz = proj_fm(W, 1536, c, g=g)
        sq = tmp("g_sq", [128, 8, 128], BF16)
        P.act(sq[:, 0:4], X[:, 0:4], AF.Square, scale=float(128 ** 0.5))
        P.act(sq[:, 4:8], X[:, 4:8], AF.Square)
        yield
        szg = tmp("g_sz", [128, 4, 128], BF16)
        P.act(szg, pz, AF.Silu)
        P.dma("sp", GZ_d[c], szg.re("p a b -> p (a b)"))
        xa = tmp("g_xa", [128, 8])
        P.tt("dve", xa[:, 0:4], ab[:, 0:4], DTB[:, 0:4], ALU.add)
        P.tt("dve", xa[:, 4:8], ab[:, 8:12], DTB[:, 4:8], ALU.add)
        SC = tmp("g_SC", [128, 56], F32, 3)
        P.act(SC[:, 40:44], ab[:, 4:8], AF.Sigmoid)
        P.act(SC[:, 44:48], ab[:, 12:16], AF.Sigmoid)
        pn1 = ps([128, 512], g=g)
        P.mm(pn1, ONESB, sq[:, 4:8].re("p a b -> p (a b)"))
        yield
        rn = tmp("g_rn", [128, 8, 128])
        P.act(rn[:, 4:8].re("p a b -> p (a b)"), pn1, AF.Ln, bias=EPS)
        ax = tmp("g_ax", [128, 8])
        P.act(ax, xa, AF.Abs)
        rx = tmp("g_rx", [128, 8])
        P.ts("dve", rx, xa, 0.0, ALU.max)
        P.ts("dve", SC[:, 32:40], SC[:, 40:48], -1.0, ALU.mult)
        pn2 = ps([128, 512], g=g)
        P.mm(pn2, ONESB, sq[:, 0:4].re("p a b -> p (a b)"))
        yield
        P.act(rn[:, 0:4].re("p a b -> p (a b)"), pn2, AF.Ln, bias=128 * EPS)
        P.act(ax, ax, AF.Exp, scale=-1.0)
        yield
        P.act(rn, rn, AF.Exp, scale=-0.5)
        P.act(ax, ax, AF.Ln, bias=1.0)
        yield
        KQ = tmp("g_KQ", [128, 4, 2, 128], BF16, 3)
        P.tt("pool", KQ[:, :, 0, :], X[:, 4:8], rn[:, 4:8], ALU.mult)
        P.tt("pool", KQ[:, :, 1, :], X[:, 0:4], rn[:, 0:4], ALU.mult)
        vTb = tmp("g_vTb", [128, 4, 128], BF16)
        P.copy("pool", vTb, X[:, 8:12])
        P.tt("dve", rx, rx, ax, ALU.add)
        yield
        gg = tmp("g_g", [128, 8])
        P.tt("dve", gg, rx, NAR, ALU.mult)
        yield
        pgc = ps([128, 16], g=g)
        P.mm(pgc[:, 0:4], TRIF, gg[:, 0:4])
        P.mm(pgc[:, 4:8], TRIB, gg[:, 4:8])
        P.mm(pgc[:, 8:16], ONESF, gg)
        yield
        P.copy("dve", SC[:, 0:16], pgc)
        ptv = transpose4(vTb, None, g=g)
        yield
        vtok = tmp("g_vtok", [128, 4, 128], BF16, 3)
        P.copy("act", vtok, ptv)
        ptk = transpose4(KQ[:, :, 0, :], None, g=g)
        P.act(SC[:, 16:24], SC[:, 0:8], AF.Exp)
        dd = tmp("g_dd", [128, 8])
        P.tt("dve", dd, SC[:, 8:16], SC[:, 0:8], ALU.subtract)
        yield
        P.act(SC[:, 24:32], dd, AF.Exp)
        P.act(SC[:, 48:56], SC[:, 8:16], AF.Exp)
        P.ts("dve", SC[:, 16:24], SC[:, 16:24], -1.0, ALU.mult)
        yield
        kdf = tmp("g_kdf", [128, 4, 128], BF16, 3)
        P.tt("dve", kdf, ptk, SC[:, 24:28].bc([128, 4, 128], axes=(2,)), ALU.mult)
        kdb = tmp("g_kdb", [128, 4, 128], BF16)
        P.tt("dve", kdb, ptk, SC[:, 28:32].bc([128, 4, 128], axes=(2,)), ALU.mult)
        yield
        P.dma("sp", GKQ_d[c], KQ.re("p a b c -> p (a b c)"))
        P.dma("sp", GVT_d[c], vtok.re("p a b -> p (a b)"))
        P.dma("sp", GKDB_d[c], kdb.re("p a b -> p (a b)"))
        P.dma("sp", GSC_d[c], SC)
        I.update(KQ=KQ, SC=SC, vtok=vtok, kdf=kdf)
        yield

    def phase_gdn_fwd(l, s, W, pre):
        new_phase(0)
        pre()
        P.memset("pool", GS, 0.0)
        P.memset("pool", GSB, 0.0)
        if PREC & 1:
            P.memset("pool", GSL, 0.0)
        g0, g1, g2, g3 = bgroup([0, 1, 2]), bgroup([3, 4, 5]), bgroup([6]), bgroup([7])
        cur = {}
        run_gens([gdn_common(0, W, cur, bgroup([6, 7]))])
        prev = None
        for c in range(nch):
            nxt = {}
            R = {}
            gens = [gdn_chain(0, c, cur, g0, R), gdn_chain(1, c, cur, g1, None)]
            if prev is not None:
                gens.append(gdn_fwd_recur(prev, g3))
            if c + 1 < nch:
                gens.append(gdn_common(c + 1, W, nxt, g2))
            run_gens(gens)
            cur = nxt
            prev = R
        run_gens([gdn_fwd_recur(prev, g3)])

    def gdn_bwd_gen(l, s, g, Q):
        P.memset("pool", GS, 0.0)
        P.memset("pool", GSB, 0.0)
        if PREC & 1:
            P.memset("pool", GSL, 0.0)
        def gloads(c):
            return (load_t("sp", "gb_KQ", [128, 4, 2, 128], BF16, GKQ_d[c]),
                    load_t("sp", "gb_vtok", [128, 4, 128], BF16, GVT_d[c]),
                    load_t("sp", "gb_kdb", [128, 4, 128], BF16, GKDB_d[c]),
                    load_t("sp", "gb_SC", [128, 56], F32, GSC_d[c]),
                    load_t("sp", "gb_P", [128, 4, 128], BF16, GPB_d[c]),
                    load_t("sp", "gb_ST", [128, 4, 128], BF16, GSTB_d[c]),
                    load_t("sp", "gb_qg", [128, 4, 128], BF16, GQGB_d[c]),
                    load_t("sp", "gb_of", [128, 4, 128], F32, GOF_d[c], 3),
                    load_t("sp", "gb_sz", [128, 4, 128], BF16, GZ_d[c], 3))

        nxt_l = gloads(nch - 1)
        for c in reversed(range(nch)):
            KQ, vtok, kdb, SC, Pm, ST, qg, of32, szg = nxt_l
            if c > 0:
                nxt_l = gloads(c - 1)
            res = {}
            for _ in gdn_recur(1, KQ, vtok, kdb, SC, Pm, ST, qg, res, g):
                yield
            assert len(Q) <= 1, "bwd norm generator fell behind the recurrence"
            Q.append((c, res["o"], of32, szg))

    def gdn_bwd_norm(l, s, g, Q):
        done = 0
        while done < nch:
            if not Q:
                yield
                continue
            c, ob, of32, szg = Q.pop(0)
            done += 1
            o = tmp("gb_o", [128, 512])
            P.tt("pool", o, ob.re("p a b -> p (a b)"), of32.re("p a b -> p (a b)"), ALU.add)
            yield
            osq = tmp("gb_osq", [128, 512])
            P.act(osq, o, AF.Square)
            yield
            pms = ps([128, 512], g=g)
            P.mm(pms, ONESDIV, osq)
            yield
            rs = osq
            P.act(rs, pms, AF.Ln, bias=EPS)
            yield
            P.act(rs, rs, AF.Exp, scale=-0.5)
            yield
            P.tt("pool", o, o, rs, ALU.mult)
            yield
            yc = tmp("gb_yc", [128, 512], BF16, 2)
            P.stt("dve", yc, o, GNG[:, 0:1], szg.re("p a b -> p (a b)"), ALU.mult, ALU.mult)
            P.dma("sp", Y_d[2][c], yc)
            yield

    def phase_bwd(l, s, pre):
        new_phase(24)
        pre()
        Q = []
        run_gens([gdn_bwd_gen(l, s, bgroup([4, 5]), Q), ret_bwd_gen(l, s, bgroup([0, 1, 2, 3])), gdn_bwd_norm(l, s, bgroup([6, 7]), Q)])

    def phase_merge(l, s, last, Wm, pres):
        new_phase(40)
        items = [(n, c) for n in range(3) for c in range(nch)]
        pl = {}
        Wsel = {}

        stored = set()

        def issue(i, force=False):
            if i >= len(items) or i in pl:
                return
            n, c = items[i]
            if n > 0 and (n - 1, c) not in stored:
                assert not force, "accumulator load issued before the producing store was recorded"
                return
            y = load_t("sp", "m_y", [128, 4, 128], BF16, Y_d[n][c], 4)
            acc = tmp("m_acc", [128, 8, 128], F32, 4)
            if n > 0:
                P.dma("sp", acc.re("p a b -> p (a b)"), ACC_d[c])
            pl[i] = (y, acc)

        issue(0)
        issue(1)

        if True:
            def gen(i, g):
                n, c = items[i]
                if c == 0:
                    Wsel[n] = Wm[n]()
                    pres[n]()
                WG, WB, WO = Wsel[n]
                issue(i, force=True)
                y, acc = pl.pop(i)
                issue(i + 2)
                mT = tmp("m_mT", [128, 8, 128], BF16, 2) if n == 2 else None
                if n == 2:
                    xt = tmp("x32", [128, 1024], F32, 2)
                    src = x_d if l == 0 else xres_d
                    P.dma("sp", xt, src[s * nch + c])
                for mh in range(2):
                    pp = ps([128, 4, 128], g=g)
                    for mt in range(4):
                        m = mh * 4 + mt
                        for kc in range(4):
                            P.mm(pp[:, mt], WB[:, kc, m * 128:(m + 1) * 128], y[:, kc],
                                 start=(kc == 0), stop=(kc == 3))
                    pgt = ps([128, 4, 128], g=g)
                    for mt in range(4):
                        m = mh * 4 + mt
                        for kc in range(8):
                            P.mm(pgt[:, mt], WG[:, kc, m * 128:(m + 1) * 128], hT[:, kc, tok(c)],
                                 start=(kc == 0), stop=(kc == 7))
                    yield
                    gs = tmp("m_gs", [128, 4, 128], F32, 2)
                    P.act(gs, pgt, AF.Sigmoid)
                    yield
                    asl = acc[:, mh * 4:(mh + 1) * 4]
                    if n == 0:
                        P.tt("dve", asl, pp, gs, ALU.mult)
                    else:
                        t = tmp("m_t", [128, 4, 128], F32, 2)
                        P.tt("dve", t, pp, gs, ALU.mult)
                        yield
                        if n == 1:
                            P.tt("pool", asl, asl, t, ALU.add)
                        else:
                            P.tt("pool", mT[:, mh * 4:(mh + 1) * 4], asl, t, ALU.add)
                    yield
                if n < 2:
                    P.dma("sp", ACC_d[c], acc.re("p a b -> p (a b)"))
                    stored.add((n, c))
                    yield
                    return
                xn = xt
                pos = []
                for nh in range(2):
                    po = ps([128, 512], g=g)
                    for kc in range(8):
                        P.mm(po, mT[:, kc], WO[:, kc, nh * 512:(nh + 1) * 512],
                             start=(kc == 0), stop=(kc == 7))
                    pos.append(po)
                yield
                for nh in range(2):
                    P.tt("dve", xn[:, nh * 512:(nh + 1) * 512], pos[nh], xt[:, nh * 512:(nh + 1) * 512], ALU.add)
                yield
                if not last:
                    P.dma("sp", xres_d[s * nch + c], xn)
                    yield
                    return
                junk = tmp("m_junk", [128, 1024], BF16, 1)
                ss = tmp("ss", [128, 1], F32, 2)
                P.act(junk, xn, AF.Square, accum=ss)
                yield
                rs = tmp("rs", [128, 1], F32, 2)
                P.act(rs, ss, AF.Ln, bias=EPS, scale=1.0 / D_MODEL)
                yield
                P.act(rs, rs, AF.Exp, scale=-0.5)
                yield
                P.stt("dve", xn, xn, rs, FNG, ALU.mult, ALU.mult)
                P.dma("sp", out_d[s * nch + c], xn)
                yield

            run_window(gen, range(len(items)), 2, 3)

    passes = [(l, s) for l in range(depth) for s in range(nseq)]
    Wn = {"gmlp": issue_w("gmlp", 0)}
    for pi, (l, s) in enumerate(passes):
        if s == 0:
            load_layer_params(l)
        last = (l == depth - 1)
        nxt = passes[pi + 1] if pi + 1 < len(passes) else None

        def pf(tag, ll=l):
            def f():
                Wn[tag] = issue_w(tag, ll)
            return f

        def pf_next():
            if nxt is not None:
                Wn["gmlp"] = issue_w("gmlp", nxt[0])

        phase0(l, s)
        phase_gmlp(l, s, Wn["gmlp"], pf("ret"))
        phase_ret_fwd(l, s, Wn["ret"], pf("gdn"))
        phase_gdn_fwd(l, s, Wn["gdn"], lambda: None)
        phase_bwd(l, s, pf("m0"))
        phase_merge(l, s, last, [lambda: Wn["m0"], lambda: Wn["m1"], lambda: Wn["m2"]],
                    [pf("m1"), pf("m2"), pf_next])

    P.emit(st)
    st.close()
    return nc, P


_PARAM_NAMES = ["norm_g", "w_in", "gm_ln_g", "gm_ln_b", "gm_w_s", "gm_b_s", "ret_decay_logit",
                "ret_norm_g", "gdn_conv_w", "gdn_a_log", "gdn_dt_bias", "gdn_norm_g", "w_gate",
                "w_branch_out", "w_out", "final_norm_g"]


def prep_params(inputs, depth):
    f = lambda a: np.ascontiguousarray(np.asarray(a, dtype=np.float32))
    p = {}
    p["norm_g"] = f(inputs["norm_g"])
    p["w_in"] = f(inputs["w_in"])
    p["gm_ln_g"] = f(inputs["gm_ln_g"])
    p["gm_ln_b"] = f(inputs["gm_ln_b"])
    p["gm_w_s"] = f(inputs["gm_w_s"])
    p["gm_b_s"] = f(inputs["gm_b_s"]).reshape(depth, 512)
    p["ret_decay_logit"] = f(inputs["ret_decay_logit"]).reshape(depth, 8)
    p["ret_norm_g"] = f(inputs["ret_norm_g"])
    p["gdn_conv_w"] = f(inputs["gdn_conv_w"])
    p["gdn_a_log"] = f(inputs["gdn_a_log"]).reshape(depth, 8)
    p["gdn_dt_bias"] = f(inputs["gdn_dt_bias"]).reshape(depth, 8)
    p["gdn_norm_g"] = f(inputs["gdn_norm_g"])
    p["w_gate"] = f(inputs["w_gate"]).reshape(depth, D_MODEL, 3072)
    p["w_branch_out"] = f(inputs["w_branch_out"]).reshape(depth, 1536, D_MODEL)
    p["w_out"] = f(inputs["w_out"])
    p["final_norm_g"] = f(inputs["final_norm_g"]).reshape(1, D_MODEL)
    return p


def run(inputs, n_cores, nseq, nch, depth):
    x = np.asarray(inputs["x"], dtype=np.float32)
    B, S, D = x.shape
    assert B == n_cores * nseq and S == nch * 128
    nc, _ = build(nseq, nch, depth)
    params = prep_params(inputs, depth)
    consts = make_consts(nch)
    in_maps = []
    for i in range(n_cores):
        m = dict(params)
        m["consts"] = consts
        m["x"] = np.ascontiguousarray(x[i * nseq:(i + 1) * nseq].reshape(nseq * nch, 128, D))
        in_maps.append(m)
    res = run_bass_kernel_spmd(nc, in_maps, core_ids=list(range(n_cores)))
    outs = [np.asarray(r["out"]).reshape(nseq, S, D) for r in res.results]
    return np.concatenate(outs, axis=0).astype(np.float32)


def kernel(**inputs):
    return run(inputs, 8, 2, 16, 2)
```
